# Optimizing a Trainium2 kernel written in Bass

```python
import jax, jax.numpy as jnp
from jax import lax
import numpy as np

D_MODEL = 1024
BATCH = 2
SEQ = 8192
DEPTH = 1

D_A = D_MODEL
D_B = D_MODEL
CONV_A_WIDTH = 3
CONV_B_WIDTH = 31
D_FF = 4 * D_MODEL
RMS_EPS = 1e-6
LN_EPS = 1e-5
N_IN = 3 * D_A + 2 * D_B + 2 * D_MODEL

kernel_name = "hybrid_shortconv_conformer_conv_block"


def rms_norm(x, g):
    xf = x.astype(jnp.float32)
    y = xf * lax.rsqrt(jnp.mean(xf * xf, axis=-1, keepdims=True) + RMS_EPS)
    return (y * g.astype(jnp.float32)).astype(x.dtype)


def layer_norm(x, g, b):
    xf = x.astype(jnp.float32)
    mu = jnp.mean(xf, axis=-1, keepdims=True)
    var = jnp.mean(jnp.square(xf - mu), axis=-1, keepdims=True)
    y = (xf - mu) * lax.rsqrt(var + LN_EPS)
    return (y * g.astype(jnp.float32) + b.astype(jnp.float32)).astype(x.dtype)


def depthwise_conv_centred(u, w, b):
    k, c = w.shape
    pad = (k - 1) // 2
    y = lax.conv_general_dilated(
        u, w[:, None, :].astype(u.dtype), window_strides=(1,), padding=[(pad, pad)],
        dimension_numbers=("NWC", "WIO", "NWC"), feature_group_count=c)
    return y + b.astype(u.dtype)


def setup_inputs(seed: int = 0) -> dict:
    key = jax.random.key(seed)
    ks = jax.random.split(key, 24)
    f32 = jnp.float32

    def nrm(k, shape, scale):
        return jax.random.normal(k, shape, f32) * scale

    def gain(k, n):
        return jnp.ones((n,), f32) + 0.05 * jax.random.normal(k, (n,), f32)

    return {
        "x": jax.random.normal(ks[0], (BATCH, SEQ, D_MODEL), f32),
        "norm1_pre_g": gain(ks[1], D_MODEL),
        "w_in": nrm(ks[2], (D_MODEL, N_IN), D_MODEL ** -0.5),
        "b_in": nrm(ks[3], (N_IN,), 0.02),
        "conv_a_w": nrm(ks[4], (CONV_A_WIDTH, D_A), CONV_A_WIDTH ** -0.5),
        "conv_a_b": nrm(ks[5], (D_A,), 0.02),
        "w_a_out": nrm(ks[6], (D_A, D_MODEL), D_A ** -0.5),
        "conv_b_w": nrm(ks[7], (CONV_B_WIDTH, D_B), CONV_B_WIDTH ** -0.5),
        "conv_b_b": nrm(ks[8], (D_B,), 0.02),
        "ln_b_g": gain(ks[9], D_B),
        "ln_b_b": nrm(ks[10], (D_B,), 0.02),
        "w_b_out": nrm(ks[11], (D_B, D_MODEL), D_B ** -0.5),
        "w_o": nrm(ks[12], (D_MODEL, D_MODEL), D_MODEL ** -0.5),
        "norm1_post_g": gain(ks[13], D_MODEL),
        "norm2_pre_g": gain(ks[14], D_MODEL),
        "w_mlp_in": nrm(ks[15], (D_MODEL, D_FF), D_MODEL ** -0.5),
        "w_mlp_out": nrm(ks[16], (D_FF, D_MODEL), D_FF ** -0.5),
        "norm2_post_g": gain(ks[17], D_MODEL),
    }


def reference(x, norm1_pre_g, w_in, b_in, conv_a_w, conv_a_b, w_a_out,
              conv_b_w, conv_b_b, ln_b_g, ln_b_b, w_b_out, w_o, norm1_post_g,
              norm2_pre_g, w_mlp_in, w_mlp_out, norm2_post_g):
    for _ in range(DEPTH):
        h = rms_norm(x, norm1_pre_g)
        proj = jnp.einsum("bsd,dn->bsn", h, w_in) + b_in
        splits = np.cumsum([D_A, D_A, D_A, D_B, D_B, D_MODEL])
        b_gate, c_gate, h_a, a_b, g_b, z_a, z_b = jnp.split(proj, splits, axis=-1)

        v_a = depthwise_conv_centred(c_gate * h_a, conv_a_w, conv_a_b)
        y_a = jnp.einsum("bsc,cd->bsd", b_gate * v_a, w_a_out)

        u_b = a_b * jax.nn.sigmoid(g_b)
        v_b = depthwise_conv_centred(u_b, conv_b_w, conv_b_b)
        v_b = jax.nn.silu(layer_norm(v_b, ln_b_g, ln_b_b))
        y_b = jnp.einsum("bsc,cd->bsd", v_b, w_b_out)

        merged = jax.nn.sigmoid(z_a) * y_a + jax.nn.sigmoid(z_b) * y_b
        mix_out = jnp.einsum("bsd,de->bse", merged, w_o)
        x = x + rms_norm(mix_out, norm1_post_g)

        h2 = rms_norm(x, norm2_pre_g)
        f = jnp.square(jax.nn.relu(jnp.einsum("bsd,df->bsf", h2, w_mlp_in)))
        f = jnp.einsum("bsf,fd->bsd", f, w_mlp_out)
        x = x + rms_norm(f, norm2_post_g)
    return x
```

```python
from contextlib import ExitStack

import numpy as np
import concourse.bass as bass
import concourse.mybir as mybir
from concourse.bass_utils import run_bass_kernel_spmd

F32 = mybir.dt.float32
BF16 = mybir.dt.bfloat16
AF = mybir.ActivationFunctionType
ALU = mybir.AluOpType

D = 1024
NT = 2048
HALO = 16
NE = NT + 2 * HALO
DFF = 4096
RMS_EPS = 1e-6
LN_EPS = 1e-5
NCOL = 112
DEBUG = False
_LAST = {}


class Buf:
    __slots__ = ("name", "writers", "readers")

    def __init__(self, name):
        self.name = name
        self.writers = []
        self.readers = []


class Op:
    __slots__ = ("eng", "idx", "fn", "deps", "is_dma", "dsem", "dval", "needs_signal", "sigval")

    def __init__(self, eng, fn, is_dma=False, dsem=None):
        self.eng = eng
        self.fn = fn
        self.deps = []
        self.is_dma = is_dma
        self.dsem = dsem
        self.dval = 0
        self.needs_signal = False
        self.sigval = 0


class Prog:
    ENGS = ("pe", "act", "dve", "pool", "sp")

    def __init__(self):
        self.ops = {e: [] for e in self.ENGS}
        self.dma_counts = {}
        self.final_waits = []

    def _add(self, op, reads, writes):
        lst = self.ops[op.eng]
        op.idx = len(lst)
        lst.append(op)
        deps = []
        same = (lambda d: (not d.is_dma) and (not op.is_dma) and d.eng == op.eng)
        for b in reads:
            deps.extend(b.writers)
        for b in writes:
            deps.extend(d for d in b.readers if not same(d))
            deps.extend(d for d in b.writers if not same(d))
        for b in reads:
            b.readers.append(op)
        for b in writes:
            if b.readers:
                b.writers = [op]
                b.readers = []
            else:
                b.writers.append(op)
        best = {}
        for d in deps:
            if d is op:
                continue
            if d.is_dma:
                key = ("dma", id(d.dsem))
                if key not in best or best[key].dval < d.dval:
                    best[key] = d
            else:
                if d.eng == "pe" and op.eng == "pe" and not op.is_dma:
                    continue
                key = ("eng", d.eng)
                if key not in best or best[key].idx < d.idx:
                    best[key] = d
        op.deps = list(best.values())
        for d in op.deps:
            d.needs_signal = True
        return op

    def op(self, eng, fn, reads=(), writes=()):
        return self._add(Op(eng, fn), list(reads), list(writes))

    def dma(self, queue, fn, sem, reads=(), writes=()):
        o = Op(queue, fn, is_dma=True, dsem=sem)
        c = self.dma_counts.get(id(sem), 0) + 16
        self.dma_counts[id(sem)] = c
        o.dval = c
        return self._add(o, list(reads), list(writes))

    @staticmethod
    def alias(new_bufs, old_bufs):
        olds = []
        for ob in old_bufs:
            olds.extend(ob.readers)
            olds.extend(ob.writers)
        for nb in new_bufs:
            nb.readers.extend(olds)

    def emit(self, block, esems):
        for e in self.ENGS:
            cnt = 0
            for o in self.ops[e]:
                if (not o.is_dma) and o.needs_signal:
                    cnt += 1
                    o.sigval = cnt
        engobj = {"pe": "tensor", "act": "scalar", "dve": "vector", "pool": "gpsimd", "sp": "sync"}
        prog = self

        def make(e):
            def body(eng):
                seen = {}
                for o in prog.ops[e]:
                    waits = {}
                    for d in o.deps:
                        if d.is_dma:
                            sem, val = d.dsem, d.dval
                        else:
                            sem, val = esems[d.eng], d.sigval
                        k = id(sem)
                        if k not in waits or waits[k][1] < val:
                            waits[k] = (sem, val)
                    for k, (sem, val) in waits.items():
                        if seen.get(k, 0) < val:
                            eng.wait_ge(sem, val)
                            seen[k] = val
                    inst = o.fn(eng)
                    if o.is_dma:
                        inst.then_inc(o.dsem, 16)
                    elif o.needs_signal:
                        inst.then_inc(esems[e], 1)
                if e == "sp":
                    for sem, val in prog.final_waits:
                        eng.wait_ge(sem, val)
            return body

        for e in self.ENGS:
            if self.ops[e] or (e == "sp" and self.final_waits):
                getattr(block, engobj[e])(make(e))


def c_bin(grp, j):
    return grp * 8 + j


def c_cbA(j):
    return 56 + j


def c_cbB(j):
    return 64 + j


def c_lng(j):
    return 72 + j


def c_lnb(j):
    return 80 + j


def c_cwA(j, tap=0):
    return 88 + j * 3 + tap


def c_cwB(j, tap=0):
    return 112 + j * 31 + tap


def build_nc():
    nc = bass.Bass("TRN2", target_bir_lowering=False, dynamic_dma_scratch_size=8192)
    x_ext = nc.dram_tensor("x_ext", [NE, D], F32, kind="ExternalInput").ap()
    hmask_d = nc.dram_tensor("hmask", [128, 32], F32, kind="ExternalInput").ap()
    w_in_r = nc.dram_tensor("w_in_r", [56, 128, 1024], F32, kind="ExternalInput").ap()
    wa_r = nc.dram_tensor("wa_r", [8, 128, 1024], F32, kind="ExternalInput").ap()
    wb_r = nc.dram_tensor("wb_r", [8, 128, 1024], F32, kind="ExternalInput").ap()
    w_o_d = nc.dram_tensor("w_o", [D, D], F32, kind="ExternalInput").ap()
    w1_r = nc.dram_tensor("w1_r", [32, 128, 1024], F32, kind="ExternalInput").ap()
    w2_r = nc.dram_tensor("w2_r", [32, 128, 1024], F32, kind="ExternalInput").ap()
    colv_d = nc.dram_tensor("colv", [128, NCOL], F32, kind="ExternalInput").ap()
    gbc_d = nc.dram_tensor("gbc", [4, 128, D], F32, kind="ExternalInput").ap()
    ident_d = nc.dram_tensor("ident", [128, 128], F32, kind="ExternalInput").ap()
    colw4_d = nc.dram_tensor("colw4", [128, 256], F32, kind="ExternalInput").ap()
    e4_d = nc.dram_tensor("e4", [128, 32], F32, kind="ExternalInput").ap()
    y = nc.dram_tensor("y", [NT, D], F32, kind="ExternalOutput").ap()
    if DEBUG:
        dbg1 = nc.dram_tensor("dbg1", [NT, D], F32, kind="ExternalOutput").ap()
        dbg2 = nc.dram_tensor("dbg2", [128, 16384], BF16, kind="ExternalOutput").ap()
        dbg3 = nc.dram_tensor("dbg3", [128, 4096], F32, kind="ExternalOutput").ap()
        dbg4 = nc.dram_tensor("dbg4", [128, 1024], F32, kind="ExternalOutput").ap()

    P = Prog()
    with ExitStack() as es:
        def sb(name, shape, dt):
            return es.enter_context(nc.sbuf_tensor("sb_" + name, shape, dt))

        def psum(name):
            return es.enter_context(nc.psum_tensor(name, [128, 512], F32))

        def sem(name):
            return es.enter_context(nc.semaphore(name))

        R1 = sb("R1", [128, 33024], BF16)
        R2 = sb("R2", [128, 16384], F32)
        R3 = sb("R3", [128, 16384], BF16)
        SC1 = sb("SC1", [128, 4096], F32)
        SC2 = sb("SC2", [128, 8192], BF16)
        WS = [sb(f"WS{i}", [128, 1024], BF16) for i in range(4)]
        GBC = sb("GBC", [128, 1024], F32)
        JUNK = sb("JUNK", [128, 1024], BF16)
        GP2 = sb("GP2", [128, 1024], F32)
        identf = sb("identf", [128, 128], F32)
        identb = sb("identb", [128, 128], BF16)
        onesb = sb("onesb", [128, 128], BF16)
        colv = sb("colv", [128, NCOL], F32)
        hmask = sb("hmask", [128, 32], F32)
        colw4 = sb("colw4", [128, 256], F32)
        e4 = sb("e4", [128, 32], F32)
        stA = sb("stA", [128, 3 * 17], F32)
        st3 = sb("st3", [128, 16 * 5], F32)
        sth = sb("sth", [128, 16 * 3], F32)
        sto = sb("sto", [128, 16 * 5], F32)

        R1f = R1.bitcast(F32)
        R2b = R2.bitcast(BF16)
        R3f = R3.bitcast(F32)
        SC1b = SC1.bitcast(BF16)
        SC2f = SC2.bitcast(F32)

        hT = R1[:, 0:8 * NE].rearrange("p (k t) -> p k t", k=8)
        pA = R1[:, 8 * NE:8 * NE + 8 * NT].rearrange("p (k t) -> p k t", k=8)
        x1 = R1f[:, 0:16 * D].rearrange("p (t d) -> p t d", t=16)
        vv = R2b[:, 16384:32768].rearrange("p (k t) -> p k t", k=8)
        xstage = R2[:, :].rearrange("p (a d) -> p a d", a=16)
        mu_bf = GBC.bitcast(BF16)
        JUNKf = JUNK.bitcast(F32)
        mT = R2b[:, 0:8 * NT].rearrange("p (k t) -> p k t", k=8)
        w_o = R3[:, 0:8192].rearrange("p (k n) -> p k n", k=8)
        gp1 = R3f[:, 4096:5120]
        gp2 = GP2[:, :]
        fT = R2b[:, :].rearrange("p (m t) -> p m t", m=32)
        U4 = R3[:, 0:4 * NE].rearrange("p (g t) -> p g t", g=4)
        W4 = [R3[:, 12480 + i * 1024:12480 + (i + 1) * 1024].rearrange("p (q c) -> p q c", q=32) for i in range(2)]
        uBs = [R3[:, 8320 + i * NE:8320 + (i + 1) * NE] for i in range(2)]
        diagA = [R3[:, 14528 + i * 384:14528 + (i + 1) * 384].rearrange("p (t n) -> p t n", t=3) for i in range(2)]
        uAs = uBs
        vB = vv
        msq_tmp = R2[:, 0:2048]
        o2T = R3f[:, :].rearrange("p (i t) -> p i t", i=8)
        acc1 = SC1[:, 0:2048]
        acc2 = SC1[:, 2048:4096]
        h2T = SC1b[:, :].rearrange("p (k t) -> p k t", k=8)

        def qf(q, n=1):
            return SC2f[:, 512 * q:512 * (q + n)]

        def qb(q, n=1):
            return SC2[:, 1024 * q:1024 * (q + n)]

        PS = [psum(f"ps{i}") for i in range(8)]
        PSb = [p.bitcast(BF16) for p in PS]
        b_ps = [Buf(f"ps{i}") for i in range(8)]
        PP = [0, 1, 2, 3]
        PC = [4, 5]
        PS1, PS2 = 6, 7

        esems = {e: sem("s_" + e) for e in ("pe", "act", "dve", "pool")}
        s_c = sem("s_c")
        s_x = [sem(f"s_x{i}") for i in range(2)]
        s_w = [sem(f"s_w{i}") for i in range(4)]
        s_wo = sem("s_wo")
        s_u4 = [sem(f"s_u4_{g}") for g in range(4)]
        s_xt = [None] + [sem(f"s_xt{i}") for i in range(1, 4)]
        s_id = sem("s_id")
        s_xa = [sem(f"s_xa{i}") for i in range(4)]
        s_g = sem("s_g")
        s_gb = sem("s_gb")
        s_o = [sem(f"s_o{i}") for i in range(3)]
        block = es.enter_context(nc.Block())

        b_const = Buf("const")
        b_idb = Buf("identb")
        b_gbc = Buf("gbc")
        b_xa = [Buf(f"xa{i}") for i in range(4)]
        b_xa0 = Buf("xa_t0")
        b_xt = [None] + [Buf(f"xa_t{i}") for i in range(1, 4)]
        b_ident = Buf("identf")
        b_gp2 = Buf("gp2")
        b_jk = [Buf("jk0"), Buf("jk1")]
        b_sq = [Buf(f"sq{i}") for i in range(4)]
        b_q = [Buf(f"q{i}") for i in range(8)]
        b_ws = [Buf(f"ws{i}") for i in range(4)]
        b_hT = [Buf(f"hT{c}") for c in range(5)]
        b_pA = [Buf(f"pA{j}") for j in range(8)]
        b_v = [[Buf(f"v{j}_{n}") for n in range(4)] for j in range(8)]
        b_uB = [Buf("uB0"), Buf("uB1")]
        b_uA = b_uB
        b_dg = [Buf(f"W4_{i}") for i in range(2)]
        b_U4 = [Buf(f"U4_{g}") for g in range(4)]
        b_dgA = [Buf(f"diagA{i}") for i in range(2)]
        b_acc = [Buf(f"acc{n}") for n in range(4)]
        b_vB = [Buf(f"vB{n}") for n in range(4)]
        b_mT = Buf("mT")
        b_wo = Buf("wo")
        b_gp = Buf("gp12")
        b_x1 = [Buf(f"x1_{t}") for t in range(16)]
        b_h2T = Buf("h2T")
        b_fT_lo = Buf("fT_lo")
        b_fT_hi = Buf("fT_hi")
        b_o2T = Buf("o2T")
        b_stA = [Buf(f"stA{i}") for i in range(17)]
        b_st3 = [Buf(f"st3_{i}") for i in range(16)]
        b_sth = [Buf(f"sth{i}") for i in range(16)]
        b_sto = [Buf(f"sto{i}") for i in range(16)]

        wcount = [0]

        def wload(src_ap):
            s = wcount[0] % 4
            wcount[0] += 1
            P.dma("pool", lambda e, s=s, src_ap=src_ap: e.dma_start(out=WS[s][:], in_=src_ap), s_w[s], writes=[b_ws[s]])
            return s, WS[s][:, :].rearrange("p (k n) -> p k n", k=8)

        ppc = [0]

        def next_pp():
            i = PP[ppc[0] % 4]
            ppc[0] += 1
            return i

        pcc = [0]

        def next_pc():
            i = PC[pcc[0] % 2]
            pcc[0] += 1
            return i

        def col(c, n=1):
            return colv[:, c:c + n]

        for a in range(4):
            P.dma("sp", lambda e, a=a: e.dma_start(out=xstage[:, a, :], in_=x_ext[128 * a:128 * a + 128, :]),
                  s_g if a == 0 else s_xt[a], reads=([] if a == 0 else [b_xa0 if a == 1 else b_xt[a - 1]]),
                  writes=[b_xa0 if a == 0 else b_xt[a]])
            if a == 0:
                P.dma("sp", lambda e: e.dma_start(out=GBC[:], in_=gbc_d[0]), s_gb, writes=[b_gbc])
                P.dma("sp", lambda e: e.dma_start(out=identf[:], in_=ident_d), s_id, writes=[b_ident])
        for g4 in range(1, 4):
            P.dma("act", lambda e, g4=g4: e.dma_start(out=xstage[:, 4 * g4:4 * g4 + 4, :],
                                                     in_=x_ext[512 * g4:512 * g4 + 512, :].rearrange("(a p) d -> p a d", p=128)),
                  s_xa[g4], writes=[b_xa[g4]])
        P.dma("act", lambda e: e.dma_start(out=qf(0, 2)[0:32, :], in_=x_ext[2048:2080, :]), s_x[0], writes=[b_q[0], b_q[1]])
        P.dma("sp", lambda e: e.dma_start(out=colv[:], in_=colv_d), s_c, reads=[b_ident], writes=[b_const])
        P.dma("sp", lambda e: e.dma_start(out=hmask[:], in_=hmask_d), s_c, writes=[b_const])
        P.dma("sp", lambda e: e.dma_start(out=colw4[:], in_=colw4_d), s_c, writes=[b_const])
        P.dma("sp", lambda e: e.dma_start(out=e4[:], in_=e4_d), s_c, writes=[b_const])
        P.dma("sp", lambda e: e.dma_start(out=GP2[:], in_=gbc_d[2]), s_x[1], writes=[b_gp2])
        P.op("dve", lambda e: e.tensor_copy(out=identb[:], in_=identf[:]), reads=[b_ident], writes=[b_idb])

        def pa_tile(i):
            nr = 128 if i < 16 else 32
            if i < 16:
                return nr, xstage[:, i, :], [b_xa0 if i == 0 else (b_xt[i] if i < 4 else b_xa[i // 4])], qb(4 + i % 4), b_q[4 + i % 4]
            return nr, qf(0, 2), [b_q[0], b_q[1]], qb(4 + i % 4), b_q[4 + i % 4]

        def pa_stage1(i):
            nr, xin, bx, xn, bxn = pa_tile(i)
            P.op("act", lambda e: e.activation(out=JUNK[0:nr, :], in_=xin[0:nr, :], func=AF.Square, accum_out=stA[0:nr, i:i + 1]),
                 reads=bx, writes=[b_stA[i]])
            P.op("act", lambda e: e.activation(out=stA[0:nr, 17 + i:18 + i], in_=stA[0:nr, i:i + 1], func=AF.Sqrt, scale=1.0 / D, bias=RMS_EPS),
                 reads=[b_stA[i]], writes=[b_stA[i]])
            P.op("dve", lambda e: e.reciprocal(out=stA[0:nr, 34 + i:35 + i], in_=stA[0:nr, 17 + i:18 + i]),
                 reads=[b_stA[i]], writes=[b_stA[i]])
            P.op("dve", lambda e: e.scalar_tensor_tensor(out=xn[0:nr, :], in0=xin[0:nr, :], scalar=stA[0:nr, 34 + i:35 + i],
                                                         in1=GBC[0:nr, :], op0=ALU.mult, op1=ALU.mult),
                 reads=bx + [b_stA[i], b_gbc], writes=[bxn])

        pa_bank = {}

        def pa_stage2(i):
            nr, xin, bx, xn, bxn = pa_tile(i)
            pi = next_pc()
            pa_bank[i] = pi
            for k in range(8):
                P.op("pe", lambda e, k=k: e.transpose(out=PSb[pi][:, k * 128:k * 128 + nr], in_=xn[0:nr, k * 128:(k + 1) * 128],
                                                      identity=identb[0:nr, 0:nr]),
                     reads=[bxn, b_idb], writes=[b_ps[pi]])

        def pa_stage3(i):
            nr = 128 if i < 16 else 32
            r0 = 128 * i
            pi = pa_bank[i]
            src = PSb[pi][:, :].rearrange("p (k t) -> p k t", k=8)[:, :, 0:nr]
            P.op("dve", lambda e: e.tensor_copy(out=hT[:, :, r0:r0 + nr], in_=src),
                 reads=[b_ps[pi]], writes=[b_hT[min(i // 4, 4)]])

        for step in range(17 + 2):
            if step < 17:
                pa_stage1(step)
            if 1 <= step < 18:
                pa_stage2(step - 1)
            if 2 <= step < 19:
                pa_stage3(step - 2)
        P.op("dve", lambda e: e.memset(onesb[:], 1.0), writes=[b_idb])
        P.op("dve", lambda e: e.memset(SC1[:], 0.0), writes=b_acc)
        Prog.alias([b for row in b_v for b in row] + [b_mT], b_xa[1:] + [b_xa0] + b_xt[1:])

        def hT_bufs(lo, hi):
            return [b_hT[c] for c in range(lo // 512, min((hi - 1) // 512, 4) + 1)]

        def inproj_group(wv, bws, lo, n):
            pi = next_pp()
            for k in range(8):
                P.op("pe", lambda e, pi=pi, k=k, wv=wv, lo=lo, n=n: e.matmul(PS[pi][:, 0:n], lhsT=wv[:, k, :], rhs=hT[:, k, lo:lo + n],
                                                                            start=(k == 0), stop=(k == 7)),
                     reads=[bws] + hT_bufs(lo, lo + n), writes=[b_ps[pi]])
            return pi

        pending_stats = []

        def emit_stats(items):
            for (j, n, sqt, vbt, bsq, bvb) in items:
                P.op("pe", lambda e, vbt=vbt: e.matmul(PS[PS1][:, :], lhsT=onesb[:], rhs=vbt, start=True, stop=True),
                     reads=[bvb, b_idb], writes=[b_ps[PS1]])
                P.op("pe", lambda e, sqt=sqt: e.matmul(PS[PS2][:, :], lhsT=onesb[:], rhs=sqt, start=True, stop=True),
                     reads=[bsq, b_idb], writes=[b_ps[PS2]])
                P.op("dve", lambda e, n=n: e.tensor_tensor(out=acc1[:, n * 512:(n + 1) * 512], in0=PS[PS1][:, :], in1=acc1[:, n * 512:(n + 1) * 512], op=ALU.add),
                     reads=[b_ps[PS1], b_acc[n]], writes=[b_acc[n]])
                P.op("dve", lambda e, n=n: e.tensor_tensor(out=acc2[:, n * 512:(n + 1) * 512], in0=PS[PS2][:, :], in1=acc2[:, n * 512:(n + 1) * 512], op=ALU.add),
                     reads=[b_ps[PS2], b_acc[n]], writes=[b_acc[n]])

        chunks = [(c * 512, 512) for c in range(4)] + [(2048, 32)]

        def pair(j, g_first, g_second, func, dst, bdst, qbase, after_chunk=None):
            s1, wv1 = wload(w_in_r[g_first * 8 + j])
            s2, wv2 = wload(w_in_r[g_second * 8 + j])
            for ci, (lo, n) in enumerate(chunks):
                if after_chunk is not None and ci > 0:
                    after_chunk(ci - 1)
                tq = qbase + ci % 2
                p1 = inproj_group(wv1, b_ws[s1], lo, n)
                P.op("act", lambda e, p1=p1, n=n, tq=tq: e.activation(out=qf(tq)[:, 0:n], in_=PS[p1][:, 0:n], func=func, bias=col(c_bin(g_first, j))),
                     reads=[b_ps[p1], b_const], writes=[b_q[tq]])
                p2 = inproj_group(wv2, b_ws[s2], lo, n)
                P.op("dve", lambda e, p2=p2, n=n, lo=lo, tq=tq: e.scalar_tensor_tensor(
                    out=dst[:, lo:lo + n], in0=PS[p2][:, 0:n], scalar=col(c_bin(g_second, j)), in1=qf(tq)[:, 0:n], op0=ALU.add, op1=ALU.mult),
                     reads=[b_ps[p2], b_q[tq], b_const], writes=[bdst])
            P.op("dve", lambda e: e.tensor_tensor(out=dst[:, 0:16], in0=dst[:, 0:16], in1=hmask[:, 0:16], op=ALU.mult),
                 reads=[bdst, b_const], writes=[bdst])
            P.op("dve", lambda e: e.tensor_tensor(out=dst[:, NE - 16:NE], in0=dst[:, NE - 16:NE], in1=hmask[:, 16:32], op=ALU.mult),
                 reads=[bdst, b_const], writes=[bdst])

        def build_diagB(j):
            jb = j % 2
            P.op("pool", lambda e: e.tensor_tensor(out=W4[jb], in0=e4[:, :].unsqueeze(1).to_broadcast([128, 32, 32]),
                                                   in1=colw4[:, j * 32:(j + 1) * 32].unsqueeze(2).to_broadcast([128, 32, 32]), op=ALU.mult),
                 reads=[b_const], writes=[b_dg[jb]])

        def u4_dma(j):
            ub_ = uBs[j % 2]
            for s_ in range(4):
                for g in range(4):
                    P.dma("sp", lambda e, g=g, s_=s_: e.dma_start(out=U4[32 * s_:32 * s_ + 32, g, 0:NE - 8 * s_], in_=ub_[32 * g:32 * g + 32, 8 * s_:NE]),
                          s_u4[g], reads=[b_uB[j % 2]], writes=[b_U4[g]])

        def build_diagA(j):
            jb = j % 2
            P.op("pool", lambda e: e.tensor_tensor(out=diagA[jb], in0=identb[:, :].unsqueeze(1).to_broadcast([128, 3, 128]),
                                                   in1=col(c_cwA(j), 3).unsqueeze(2).to_broadcast([128, 3, 128]), op=ALU.mult),
                 reads=[b_idb, b_const], writes=[b_dgA[jb]])

        def pairB(j, after_chunk=None):
            pair(j, 4, 3, AF.Sigmoid, uBs[j % 2], b_uB[j % 2], 2, after_chunk)

        def pairA(j, after_chunk=None):
            pair(j, 2, 1, AF.Identity, uAs[j % 2], b_uA[j % 2], 0, after_chunk)

        def conv31(j):
            jb = j % 2
            items = []
            for n in range(4):
                e0 = HALO + 512 * n
                pc = next_pc()
                for tap0 in range(8):
                    for g in range(4):
                        P.op("pe", lambda e, pc=pc, tap0=tap0, g=g, e0=e0: e.matmul(PS[pc][32 * g:32 * g + 32, :], lhsT=W4[jb][:, g * 8 + tap0, :],
                                                                                 rhs=U4[:, g, e0 + tap0 - 15:e0 + tap0 - 15 + 512],
                                                                                 start=(tap0 == 0), stop=(tap0 == 7), tile_position=(0, 32 * g)),
                             reads=[b_dg[jb], b_U4[g]], writes=[b_ps[pc]])
                vt = vv[:, j, n * 512:(n + 1) * 512]
                sqt = qb(6 + n // 2)[:, (n % 2) * 512:(n % 2) * 512 + 512]
                P.op("act", lambda e, pc=pc, vt=vt: e.activation(out=vt, in_=PS[pc][:, :], func=AF.Identity, bias=col(c_cbB(j))),
                     reads=[b_ps[pc], b_const], writes=[b_v[j][n]])
                P.op("act", lambda e, pc=pc, sqt=sqt: e.activation(out=sqt, in_=PS[pc][:, :], func=AF.Square, bias=col(c_cbB(j))),
                     reads=[b_ps[pc], b_const], writes=[b_sq[n]])
                items.append((j, n, sqt, vt, b_sq[n], b_v[j][n]))
            return items

        def conv3_bg(j, after_chunk=None):
            jb = j % 2
            uA = uAs[jb]
            s0, wv0 = wload(w_in_r[0 * 8 + j])
            for n in range(4):
                e0 = HALO + 512 * n
                pc = next_pc()
                for tap in range(3):
                    P.op("pe", lambda e, pc=pc, tap=tap, e0=e0: e.matmul(PS[pc][:, :], lhsT=diagA[jb][:, tap, :], rhs=uA[:, e0 + tap - 1:e0 + tap - 1 + 512],
                                                                      start=(tap == 0), stop=(tap == 2)),
                         reads=[b_dgA[jb], b_uA[jb]], writes=[b_ps[pc]])
                tq = 4 + n % 2
                P.op("act", lambda e, pc=pc, tq=tq: e.activation(out=qf(tq), in_=PS[pc][:, :], func=AF.Identity, bias=col(c_cbA(j))),
                     reads=[b_ps[pc], b_const], writes=[b_q[tq]])
                p0 = inproj_group(wv0, b_ws[s0], e0, 512)
                P.op("dve", lambda e, p0=p0, tq=tq, n=n: e.scalar_tensor_tensor(out=pA[:, j, n * 512:(n + 1) * 512], in0=PS[p0][:, :], scalar=col(c_bin(0, j)),
                                                                             in1=qf(tq), op0=ALU.add, op1=ALU.mult),
                     reads=[b_ps[p0], b_q[tq], b_const], writes=[b_pA[j]])
                if after_chunk is not None:
                    after_chunk()

        pairB(0)
        build_diagB(0)
        u4_dma(0)
        prev_items = None
        for j in range(8):
            if j + 1 < 8:
                pi_ = prev_items
                pairB(j + 1, after_chunk=(lambda ci, pi_=pi_: emit_stats([pi_[ci]])) if pi_ else None)
                build_diagB(j + 1)
            else:
                pi_ = prev_items
                pairA(0, after_chunk=lambda ci, pi_=pi_: emit_stats([pi_[ci]]))
                build_diagA(0)
            prev_items = conv31(j)
            if j + 1 < 8:
                u4_dma(j + 1)
        last_items = prev_items

        bg = []

        def fin_a(n, step):
            a1 = acc1[:, n * 512:(n + 1) * 512]
            a2 = acc2[:, n * 512:(n + 1) * 512]
            mq = msq_tmp[:, n * 512:(n + 1) * 512]
            if step == 0:
                P.op("dve", lambda e: e.tensor_scalar(out=a1, in0=a1, scalar1=1.0 / D, scalar2=None, op0=ALU.mult), reads=[b_acc[n]], writes=[b_acc[n]])
            elif step == 1:
                P.op("dve", lambda e: e.tensor_tensor(out=mq, in0=a1, in1=a1, op=ALU.mult), reads=[b_acc[n]], writes=[b_mT])
            elif step == 2:
                P.op("dve", lambda e: e.scalar_tensor_tensor(out=a2, in0=a2, scalar=1.0 / D, in1=mq, op0=ALU.mult, op1=ALU.subtract),
                     reads=[b_acc[n], b_mT], writes=[b_acc[n]])
            else:
                P.op("act", lambda e: e.activation(out=a2, in_=a2, func=AF.Sqrt, bias=LN_EPS), reads=[b_acc[n]], writes=[b_acc[n]])

        def fin_n(n):
            a1 = acc1[:, n * 512:(n + 1) * 512]
            a2 = acc2[:, n * 512:(n + 1) * 512]
            mb = mu_bf[:, n * 512:(n + 1) * 512]
            rb = SC1b[:, n * 1024:n * 1024 + 512]
            P.op("dve", lambda e: e.tensor_copy(out=mb, in_=a1), reads=[b_acc[n]], writes=[b_gbc])
            P.op("dve", lambda e: e.tensor_copy(out=rb, in_=a2), reads=[b_acc[n], b_gbc], writes=[b_acc[n]])

        def ln_unit(n, j, part):
            mb = mu_bf[:, n * 512:(n + 1) * 512]
            rb = SC1b[:, n * 1024:n * 1024 + 512]
            tq = j % 2
            tt_ = JUNK[:, tq * 512:(tq + 1) * 512]
            vt = vv[:, j, n * 512:(n + 1) * 512]
            if part == 0:
                P.op("dve", lambda e: e.tensor_tensor(out=tt_, in0=vt, in1=mb, op=ALU.subtract), reads=[b_v[j][n], b_gbc], writes=[b_jk[tq]])
                P.op("dve", lambda e: e.tensor_tensor(out=tt_, in0=tt_, in1=rb, op=ALU.mult), reads=[b_jk[tq], b_acc[n]], writes=[b_jk[tq]])
            else:
                P.op("act", lambda e: e.activation(out=vt, in_=tt_, func=AF.Silu, scale=col(c_lng(j)), bias=col(c_lnb(j))),
                     reads=[b_jk[tq], b_const], writes=[b_v[j][n]])

        def recip_half(n, hh):
            a2h = acc2[:, n * 512 + hh * 256:n * 512 + (hh + 1) * 256]
            P.op("dve", lambda e: e.reciprocal(out=a2h, in_=a2h), reads=[b_acc[n]], writes=[b_acc[n]])

        for n in range(4):
            for step in range(4):
                bg.append(lambda n=n, step=step: fin_a(n, step))
            bg.append(lambda n=n: recip_half(n, 0))
            bg.append(lambda n=n: recip_half(n, 1))
            bg.append(lambda n=n: fin_n(n))
            for j in range(9):
                if j < 8:
                    bg.append(lambda n=n, j=j: ln_unit(n, j, 0))
                if j >= 1:
                    bg.append(lambda n=n, j=j: ln_unit(n, j - 1, 1))

        def drain(k):
            for _ in range(k):
                if bg:
                    bg.pop(0)()

        for j in range(8):
            if j + 1 < 8:
                if j == 0:
                    pairA(1, after_chunk=lambda ci: emit_stats([last_items[ci]]))
                else:
                    pairA(j + 1)
                build_diagA(j + 1)
            if j == 0:
                drain(8)
            conv3_bg(j, after_chunk=lambda: drain(3))
        drain(len(bg))

        Prog.alias([b_q[6], b_q[7]], b_sq)
        allv = [b for row in b_v for b in row]
        wo_pending = []
        oldB = b_uB + b_dg + b_dgA + b_U4
        for i in range(8):
            if i == 1 and DEBUG:
                s_d2 = sem("s_dbg2")
                P.dma("sp", lambda e: e.dma_start(out=dbg2, in_=R3[:, :]), s_d2, reads=allv)
                P.dma("sp", lambda e: e.dma_start(out=dbg3, in_=SC1[:, :]), s_d2, reads=allv + b_acc)
                P.dma("sp", lambda e: e.dma_start(out=dbg4, in_=GBC[:, :]), s_d2, reads=allv + [b_gbc])
            if i == 2:
                Prog.alias([b_wo, b_gp], oldB)
                wo_pending.extend(range(8))
                P.dma("sp", lambda e: e.dma_start(out=gp1, in_=gbc_d[1]), s_g, writes=[b_gp])
            for (zg, wsrc, act_src, bsrc_of, first) in ((5, wa_r, pA, lambda n: b_pA, True), (6, wb_r, vB, lambda n: [b_v[jj][n] for jj in range(8)], False)):
                sz, wvz = wload(w_in_r[zg * 8 + i])
                sy, wvy = wload(wsrc[i])
                for _ in range(2):
                    if wo_pending:
                        k_ = wo_pending.pop(0)
                        P.dma("pool", lambda e, k=k_: e.dma_start(out=w_o[:, k, :], in_=w_o_d[k * 128:(k + 1) * 128, :]), s_wo, writes=[b_wo])
                for n in range(4):
                    e0 = HALO + 512 * n
                    sq_ = n % 2
                    pz = inproj_group(wvz, b_ws[sz], e0, 512)
                    P.op("act", lambda e, pz=pz, sq_=sq_, zg=zg, i=i: e.activation(out=qf(sq_), in_=PS[pz][:, :], func=AF.Sigmoid, bias=col(c_bin(zg, i))),
                         reads=[b_ps[pz], b_const], writes=[b_q[sq_]])
                    py = next_pp()
                    for k in range(8):
                        P.op("pe", lambda e, py=py, k=k, wvy=wvy, act_src=act_src, n=n: e.matmul(PS[py][:, :], lhsT=wvy[:, k, :], rhs=act_src[:, k, n * 512:(n + 1) * 512],
                                                                                           start=(k == 0), stop=(k == 7)),
                             reads=[b_ws[sy]] + bsrc_of(n), writes=[b_ps[py]])
                    if first:
                        P.op("dve", lambda e, py=py, sq_=sq_, n=n: e.tensor_tensor(out=qf(2 + n), in0=PS[py][:, :], in1=qf(sq_), op=ALU.mult),
                             reads=[b_ps[py], b_q[sq_]], writes=[b_q[2 + n]])
                    else:
                        P.op("dve", lambda e, py=py, sq_=sq_, n=n: e.tensor_tensor(out=qf(6 + sq_), in0=PS[py][:, :], in1=qf(sq_), op=ALU.mult),
                             reads=[b_ps[py], b_q[sq_]], writes=[b_q[6 + sq_]])
                        P.op("dve", lambda e, sq_=sq_, n=n, i=i: e.tensor_tensor(out=mT[:, i, n * 512:(n + 1) * 512], in0=qf(6 + sq_), in1=qf(2 + n), op=ALU.add),
                             reads=[b_q[6 + sq_], b_q[2 + n]], writes=[b_mT])

        Prog.alias(b_x1, b_hT + b_pA)
        Prog.alias([b_h2T], b_acc)

        def H2_elem(tt):
            hq = 6 + tt % 2
            x1t = x1[:, tt, :]
            P.op("act", lambda e, x1t=x1t, tt=tt: e.activation(out=JUNK[:, :], in_=x1t, func=AF.Square, accum_out=sth[:, tt:tt + 1]),
                 reads=[b_x1[tt]], writes=[b_sth[tt]])
            P.op("act", lambda e, tt=tt: e.activation(out=sth[:, 16 + tt:17 + tt], in_=sth[:, tt:tt + 1], func=AF.Sqrt, scale=1.0 / D, bias=RMS_EPS),
                 reads=[b_sth[tt]], writes=[b_sth[tt]])
            P.op("dve", lambda e, tt=tt: e.reciprocal(out=sth[:, 32 + tt:33 + tt], in_=sth[:, 16 + tt:17 + tt]), reads=[b_sth[tt]], writes=[b_sth[tt]])
            P.op("dve", lambda e, x1t=x1t, tt=tt, hq=hq: e.scalar_tensor_tensor(out=qb(hq), in0=x1t, scalar=sth[:, 32 + tt:33 + tt], in1=gp2, op0=ALU.mult, op1=ALU.mult),
                 reads=[b_x1[tt], b_sth[tt], b_gp2], writes=[b_q[hq]])

        def H2_tr(tt):
            ttl = tt % 8
            hq = 6 + tt % 2
            pi = next_pc()
            for k in range(8):
                P.op("pe", lambda e, pi=pi, k=k, hq=hq: e.transpose(out=PSb[pi][:, k * 128:(k + 1) * 128], in_=qb(hq)[:, k * 128:(k + 1) * 128], identity=identb[:]),
                     reads=[b_q[hq], b_idb], writes=[b_ps[pi]])
            P.op("act", lambda e, pi=pi, ttl=ttl: e.activation(out=h2T[:, :, ttl * 128:(ttl + 1) * 128], in_=PSb[pi][:, :].rearrange("p (k t) -> p k t", k=8), func=AF.Copy),
                 reads=[b_ps[pi]], writes=[b_h2T])

        def s3_step(tt):
            s = tt % 2
            xin = qf(2 * s, 2)
            bx = [b_q[2 * s], b_q[2 * s + 1]]
            P.dma("sp", lambda e, xin=xin, tt=tt: e.dma_start(out=xin, in_=x_ext[HALO + tt * 128:HALO + (tt + 1) * 128, :]), s_x[s], writes=bx)
            pis = []
            for nh in range(2):
                pi = next_pp()
                pis.append(pi)
                for k in range(8):
                    P.op("pe", lambda e, pi=pi, k=k, tt=tt, nh=nh: e.matmul(PS[pi][:, :], lhsT=mT[:, k, tt * 128:(tt + 1) * 128], rhs=w_o[:, k, nh * 512:(nh + 1) * 512],
                                                                         start=(k == 0), stop=(k == 7)),
                         reads=[b_mT, b_wo], writes=[b_ps[pi]])
                P.op("act", lambda e, pi=pi, tt=tt, nh=nh: e.activation(out=JUNK[:, 0:512], in_=PS[pi][:, :], func=AF.Square, accum_out=st3[:, 2 * tt + nh:2 * tt + nh + 1]),
                     reads=[b_ps[pi]], writes=[b_st3[tt]])
            P.op("dve", lambda e, tt=tt: e.tensor_tensor(out=st3[:, 32 + tt:33 + tt], in0=st3[:, 2 * tt:2 * tt + 1], in1=st3[:, 2 * tt + 1:2 * tt + 2], op=ALU.add),
                 reads=[b_st3[tt]], writes=[b_st3[tt]])
            P.op("act", lambda e, tt=tt: e.activation(out=st3[:, 48 + tt:49 + tt], in_=st3[:, 32 + tt:33 + tt], func=AF.Sqrt, scale=1.0 / D, bias=RMS_EPS),
                 reads=[b_st3[tt]], writes=[b_st3[tt]])
            P.op("dve", lambda e, tt=tt: e.reciprocal(out=st3[:, 64 + tt:65 + tt], in_=st3[:, 48 + tt:49 + tt]), reads=[b_st3[tt]], writes=[b_st3[tt]])
            for nh in range(2):
                P.op("dve", lambda e, pi=pis[nh], tt=tt, nh=nh: e.scalar_tensor_tensor(out=x1[:, tt, nh * 512:(nh + 1) * 512], in0=PS[pi][:, :], scalar=st3[:, 64 + tt:65 + tt],
                                                                                in1=gp1[:, nh * 512:(nh + 1) * 512], op0=ALU.mult, op1=ALU.mult),
                     reads=[b_ps[pis[nh]], b_st3[tt], b_gp], writes=[b_x1[tt]])
            P.op("dve", lambda e, tt=tt, xin=xin: e.tensor_tensor(out=x1[:, tt, :], in0=x1[:, tt, :], in1=xin, op=ALU.add),
                 reads=[b_x1[tt]] + bx, writes=[b_x1[tt]])
            if 1 <= tt <= 8:
                H2_elem(tt - 1)
            if 2 <= tt <= 9:
                H2_tr(tt - 2)


        for tt in range(10):
            s3_step(tt)

        if DEBUG:
            s_d = sem("s_dbg")
            for tt in range(16):
                P.dma("sp", lambda e, tt=tt: e.dma_start(out=dbg1[tt * 128:(tt + 1) * 128, :], in_=x1[:, tt, :]), s_d, reads=[b_x1[tt]])
        P.dma("sp", lambda e: e.dma_start(out=GBC[:], in_=gbc_d[3]), s_gb, writes=[b_gbc])

        def mlp_in_block(h, m):
            s, wv = wload(w1_r[m])
            for n in range(2):
                pi = next_pp()
                for k in range(8):
                    P.op("pe", lambda e, pi=pi, k=k, wv=wv, n=n: e.matmul(PS[pi][:, :], lhsT=wv[:, k, :], rhs=h2T[:, k, n * 512:(n + 1) * 512], start=(k == 0), stop=(k == 7)),
                         reads=[b_ws[s], b_h2T], writes=[b_ps[pi]])
                rq = 6 + n
                P.op("act", lambda e, pi=pi, rq=rq: e.activation(out=qf(rq), in_=PS[pi][:, :], func=AF.Relu), reads=[b_ps[pi]], writes=[b_q[rq]])
                P.op("dve", lambda e, rq=rq, m=m, n=n: e.tensor_tensor(out=fT[:, m, n * 512:(n + 1) * 512], in0=qf(rq), in1=qf(rq), op=ALU.mult),
                     reads=[b_q[rq]], writes=[b_fT_hi if m >= 16 else b_fT_lo])

        def mlp_out_mm(h, hook=None, early_tail=None):
            for i in range(8):
                pis = [PP[(i % 2) * 2 + n] for n in range(2)]
                if early_tail is not None and i == 7:
                    pieces = [wload(w2_r[i * 4 + kq]) for kq in range(4)]
                    for n in range(2):
                        for kq in range(4):
                            s, wv = pieces[kq]
                            for kk in range(8):
                                P.op("pe", lambda e, pi=pis[n], wv=wv, kk=kk, kq=kq, n=n: e.matmul(PS[pi][:, :], lhsT=wv[:, kk, :], rhs=fT[:, kq * 8 + kk, n * 512:(n + 1) * 512],
                                                                                               start=(kq == 0 and kk == 0), stop=(kq == 3 and kk == 7)),
                                     reads=[b_ws[s], b_fT_lo, b_fT_hi], writes=[b_ps[pis[n]]])
                            if n == 1:
                                early_tail(kq)
                        if n == 0:
                            P.op("act", lambda e, pi=pis[0], i=i: e.activation(out=o2T[:, i, 0:512], in_=PS[pi][:, :], func=AF.Copy),
                                 reads=[b_ps[pis[0]]], writes=[b_o2T])
                    P.op("act", lambda e, pi=pis[1], i=i: e.activation(out=o2T[:, i, 512:1024], in_=PS[pi][:, :], func=AF.Copy),
                         reads=[b_ps[pis[1]]], writes=[b_o2T])
                    continue
                for kq in range(4):
                    s, wv = wload(w2_r[i * 4 + kq])
                    for n in range(2):
                        for kk in range(8):
                            P.op("pe", lambda e, pi=pis[n], wv=wv, kk=kk, kq=kq, n=n: e.matmul(PS[pi][:, :], lhsT=wv[:, kk, :], rhs=fT[:, kq * 8 + kk, n * 512:(n + 1) * 512],
                                                                                           start=(kq == 0 and kk == 0), stop=(kq == 3 and kk == 7)),
                                 reads=[b_ws[s], b_fT_lo, b_fT_hi], writes=[b_ps[pis[n]]])
                for n in range(2):
                    P.op("act", lambda e, pi=pis[n], i=i, n=n: e.activation(out=o2T[:, i, n * 512:(n + 1) * 512], in_=PS[pi][:, :], func=AF.Copy),
                         reads=[b_ps[pis[n]]], writes=[b_o2T])
                if hook is not None:
                    hook(i)

        def mlp_out_tail(h, ttl):
            tt = h * 8 + ttl
            banks = [PC[0], PC[1]] if ttl % 2 == 0 else [PS1, PS2]
            for i in range(8):
                pi = banks[i // 4]
                P.op("pe", lambda e, pi=pi, i=i, ttl=ttl: e.transpose(out=PS[pi][:, (i % 4) * 128:(i % 4 + 1) * 128], in_=o2T[:, i, ttl * 128:(ttl + 1) * 128], identity=identf[:]),
                     reads=[b_o2T, b_ident], writes=[b_ps[pi]])
            for x in range(2):
                P.op("act", lambda e, pi=banks[x], tt=tt, x=x: e.activation(out=JUNK[:, 0:512], in_=PS[pi][:, :], func=AF.Square, accum_out=sto[:, 2 * tt + x:2 * tt + x + 1]),
                     reads=[b_ps[banks[x]]], writes=[b_sto[tt]])
            P.op("dve", lambda e, tt=tt: e.tensor_tensor(out=sto[:, 32 + tt:33 + tt], in0=sto[:, 2 * tt:2 * tt + 1], in1=sto[:, 2 * tt + 1:2 * tt + 2], op=ALU.add),
                 reads=[b_sto[tt]], writes=[b_sto[tt]])
            P.op("act", lambda e, tt=tt: e.activation(out=sto[:, 48 + tt:49 + tt], in_=sto[:, 32 + tt:33 + tt], func=AF.Sqrt, scale=1.0 / D, bias=RMS_EPS),
                 reads=[b_sto[tt]], writes=[b_sto[tt]])
            P.op("dve", lambda e, tt=tt: e.reciprocal(out=sto[:, 64 + tt:65 + tt], in_=sto[:, 48 + tt:49 + tt]), reads=[b_sto[tt]], writes=[b_sto[tt]])
            s = tt % 3
            yout = qf(2 * s, 2)
            by = [b_q[2 * s], b_q[2 * s + 1]]
            for x in range(2):
                P.op("dve", lambda e, pi=banks[x], tt=tt, x=x, yout=yout: e.scalar_tensor_tensor(out=yout[:, x * 512:(x + 1) * 512], in0=PS[pi][:, :], scalar=sto[:, 64 + tt:65 + tt],
                                                                                          in1=GBC[:, x * 512:(x + 1) * 512], op0=ALU.mult, op1=ALU.mult),
                     reads=[b_ps[banks[x]], b_sto[tt], b_gbc], writes=by)
            P.op("pool" if h == 1 else "dve", lambda e, tt=tt, yout=yout: e.tensor_tensor(out=yout, in0=yout, in1=x1[:, tt, :], op=ALU.add),
                 reads=by + [b_x1[tt]], writes=by)
            P.dma("sp", lambda e, tt=tt, yout=yout: e.dma_start(out=y[tt * 128:(tt + 1) * 128, :], in_=yout), s_o[s], reads=by)

        Prog.alias([b_fT_hi], allv)
        hi_blocks = list(range(16, 32))
        for tt in range(10, 16):
            s3_step(tt)
            for _ in range(3):
                if hi_blocks:
                    mlp_in_block(0, hi_blocks.pop(0))
        Prog.alias([b_fT_lo], [b_mT])
        Prog.alias([b_o2T], [b_wo, b_gp])
        for m in hi_blocks + list(range(16)):
            mlp_in_block(0, m)
        def hook0(i):
            if 8 + i + 1 <= 15:
                H2_elem(8 + i + 1)
            H2_tr(8 + i)

        H2_elem(8)
        mlp_out_mm(0, hook=hook0)
        for m in range(32):
            mlp_in_block(1, m)
            if 4 <= m < 12:
                mlp_out_tail(0, m - 4)
        mlp_out_mm(1, early_tail=lambda kq: mlp_out_tail(1, kq))
        for ttl in range(4, 8):
            mlp_out_tail(1, ttl)

        for i in range(3):
            P.final_waits.append((s_o[i], P.dma_counts[id(s_o[i])]))
        if DEBUG:
            P.final_waits.append((s_d, P.dma_counts[id(s_d)]))
            P.final_waits.append((s_d2, P.dma_counts[id(s_d2)]))
        P.emit(block, esems)
    return nc


def _blocks(w, nblk):
    return np.ascontiguousarray(w.reshape(8, 128, nblk, 128).transpose(2, 1, 0, 3).reshape(nblk, 128, 1024))


def _colmajor(v):
    return v.reshape(-1, 128).T


_NC_CACHE = {}


def kernel(x, norm1_pre_g, w_in, b_in, conv_a_w, conv_a_b, w_a_out, conv_b_w, conv_b_b, ln_b_g, ln_b_b,
           w_b_out, w_o, norm1_post_g, norm2_pre_g, w_mlp_in, w_mlp_out, norm2_post_g):
    f = lambda a: np.asarray(a, dtype=np.float32)
    x = f(x)
    w_in = f(w_in)
    B, S, _ = x.shape
    w_in_r = np.ascontiguousarray(w_in.reshape(8, 128, 7, 8, 128).transpose(2, 3, 1, 0, 4).reshape(56, 128, 1024))
    wa_r = _blocks(f(w_a_out), 8)
    wb_r = _blocks(f(w_b_out), 8)
    w1_r = _blocks(f(w_mlp_in), 32)
    w2_r = np.ascontiguousarray(f(w_mlp_out).reshape(4, 8, 128, 8, 128).transpose(3, 0, 2, 1, 4).reshape(32, 128, 1024))
    colv = np.zeros((128, NCOL), np.float32)
    colv[:, 0:56] = _colmajor(f(b_in))
    colv[:, 56:64] = _colmajor(f(conv_a_b))
    colv[:, 64:72] = _colmajor(f(conv_b_b))
    colv[:, 72:80] = _colmajor(f(ln_b_g))
    colv[:, 80:88] = _colmajor(f(ln_b_b))
    cwa = f(conv_a_w)
    cwb = f(conv_b_w)
    colv[:, 88:112] = cwa.reshape(3, 8, 128).transpose(2, 1, 0).reshape(128, 24)
    cwb_pad = np.concatenate([cwb, np.zeros((1, D), np.float32)], axis=0)
    colw4 = np.ascontiguousarray(cwb_pad.reshape(4, 8, 8, 4, 32).transpose(0, 4, 2, 3, 1).reshape(128, 256))
    e4 = np.ascontiguousarray(np.tile(np.eye(32, dtype=np.float32), (4, 1)))
    gbc = np.ascontiguousarray(np.broadcast_to(
        np.stack([f(norm1_pre_g), f(norm1_post_g), f(norm2_pre_g), f(norm2_post_g)])[:, None, :], (4, 128, D)))
    ident = np.eye(128, dtype=np.float32)
    w_o = np.ascontiguousarray(f(w_o))

    in_maps = []
    nq = S // NT
    for c in range(8):
        b, q = c // nq, c % nq
        t0 = q * NT
        xe = np.zeros((NE, D), np.float32)
        lo, hi = max(t0 - HALO, 0), min(t0 + NT + HALO, S)
        xe[lo - (t0 - HALO):hi - (t0 - HALO)] = x[b, lo:hi]
        hm = np.zeros((128, 32), np.float32)
        if q > 0:
            hm[:, 0:16] = 1.0
        if q < nq - 1:
            hm[:, 16:32] = 1.0
        in_maps.append({"x_ext": xe, "hmask": hm, "w_in_r": w_in_r, "wa_r": wa_r, "wb_r": wb_r, "w_o": w_o,
                        "w1_r": w1_r, "w2_r": w2_r, "colv": colv, "gbc": gbc, "ident": ident, "colw4": colw4, "e4": e4})
    if "nc" not in _NC_CACHE:
        _NC_CACHE["nc"] = build_nc()
    nc = _NC_CACHE["nc"]
    res = run_bass_kernel_spmd(nc, in_maps, core_ids=list(range(8)))
    if DEBUG:
        _LAST["res"] = res
    out = np.empty((B, S, D), np.float32)
    for c in range(8):
        b, q = c // nq, c % nq
        out[b, q * NT:(q + 1) * NT] = res.results[c]["y"]
    return out
```

```python
from contextlib import ExitStack

import numpy as np
import concourse.bass as bass
import concourse.mybir as mybir
from concourse.bass_utils import run_bass_kernel_spmd

F32 = mybir.dt.float32
BF16 = mybir.dt.bfloat16
AF = mybir.ActivationFunctionType
ALU = mybir.AluOpType

D = 1024
NT = 2048
HALO = 16
NE = NT + 2 * HALO
DFF = 4096
RMS_EPS = 1e-6
LN_EPS = 1e-5
NCOL = 112
DEBUG = False
_LAST = {}


class Buf:
    __slots__ = ("name", "writers", "readers")

    def __init__(self, name):
        self.name = name
        self.writers = []
        self.readers = []


class Op:
    __slots__ = ("eng", "idx", "fn", "deps", "is_dma", "dsem", "dval", "needs_signal", "sigval")

    def __init__(self, eng, fn, is_dma=False, dsem=None):
        self.eng = eng
        self.fn = fn
        self.deps = []
        self.is_dma = is_dma
        self.dsem = dsem
        self.dval = 0
        self.needs_signal = False
        self.sigval = 0


class Prog:
    ENGS = ("pe", "act", "dve", "pool", "sp")

    def __init__(self):
        self.ops = {e: [] for e in self.ENGS}
        self.dma_counts = {}
        self.final_waits = []

    def _add(self, op, reads, writes):
        lst = self.ops[op.eng]
        op.idx = len(lst)
        lst.append(op)
        deps = []
        same = (lambda d: (not d.is_dma) and (not op.is_dma) and d.eng == op.eng)
        for b in reads:
            deps.extend(b.writers)
        for b in writes:
            deps.extend(d for d in b.readers if not same(d))
            deps.extend(d for d in b.writers if not same(d))
        for b in reads:
            b.readers.append(op)
        for b in writes:
            if b.readers:
                b.writers = [op]
                b.readers = []
            else:
                b.writers.append(op)
        best = {}
        for d in deps:
            if d is op:
                continue
            if d.is_dma:
                key = ("dma", id(d.dsem))
                if key not in best or best[key].dval < d.dval:
                    best[key] = d
            else:
                if d.eng == "pe" and op.eng == "pe" and not op.is_dma:
                    continue
                key = ("eng", d.eng)
                if key not in best or best[key].idx < d.idx:
                    best[key] = d
        op.deps = list(best.values())
        for d in op.deps:
            d.needs_signal = True
        return op

    def op(self, eng, fn, reads=(), writes=()):
        return self._add(Op(eng, fn), list(reads), list(writes))

    def dma(self, queue, fn, sem, reads=(), writes=()):
        o = Op(queue, fn, is_dma=True, dsem=sem)
        c = self.dma_counts.get(id(sem), 0) + 16
        self.dma_counts[id(sem)] = c
        o.dval = c
        return self._add(o, list(reads), list(writes))

    @staticmethod
    def alias(new_bufs, old_bufs):
        olds = []
        for ob in old_bufs:
            olds.extend(ob.readers)
            olds.extend(ob.writers)
        for nb in new_bufs:
            nb.readers.extend(olds)

    def emit(self, block, esems):
        for e in self.ENGS:
            cnt = 0
            for o in self.ops[e]:
                if (not o.is_dma) and o.needs_signal:
                    cnt += 1
                    o.sigval = cnt
        engobj = {"pe": "tensor", "act": "scalar", "dve": "vector", "pool": "gpsimd", "sp": "sync"}
        prog = self

        def make(e):
            def body(eng):
                seen = {}
                for o in prog.ops[e]:
                    waits = {}
                    for d in o.deps:
                        if d.is_dma:
                            sem, val = d.dsem, d.dval
                        else:
                            sem, val = esems[d.eng], d.sigval
                        k = id(sem)
                        if k not in waits or waits[k][1] < val:
                            waits[k] = (sem, val)
                    for k, (sem, val) in waits.items():
                        if seen.get(k, 0) < val:
                            eng.wait_ge(sem, val)
                            seen[k] = val
                    inst = o.fn(eng)
                    if o.is_dma:
                        inst.then_inc(o.dsem, 16)
                    elif o.needs_signal:
                        inst.then_inc(esems[e], 1)
                if e == "sp":
                    for sem, val in prog.final_waits:
                        eng.wait_ge(sem, val)
            return body

        for e in self.ENGS:
            if self.ops[e] or (e == "sp" and self.final_waits):
                getattr(block, engobj[e])(make(e))


def c_bin(grp, j):
    return grp * 8 + j


def c_cbA(j):
    return 56 + j


def c_cbB(j):
    return 64 + j


def c_lng(j):
    return 72 + j


def c_lnb(j):
    return 80 + j


def c_cwA(j, tap=0):
    return 88 + j * 3 + tap


def c_cwB(j, tap=0):
    return 112 + j * 31 + tap


def build_nc():
    nc = bass.Bass("TRN2", target_bir_lowering=False, dynamic_dma_scratch_size=8192)
    x_ext = nc.dram_tensor("x_ext", [NE, D], F32, kind="ExternalInput").ap()
    hmask_d = nc.dram_tensor("hmask", [128, 32], F32, kind="ExternalInput").ap()
    w_in_r = nc.dram_tensor("w_in_r", [56, 128, 1024], F32, kind="ExternalInput").ap()
    wa_r = nc.dram_tensor("wa_r", [8, 128, 1024], F32, kind="ExternalInput").ap()
    wb_r = nc.dram_tensor("wb_r", [8, 128, 1024], F32, kind="ExternalInput").ap()
    w_o_d = nc.dram_tensor("w_o", [D, D], F32, kind="ExternalInput").ap()
    w1_r = nc.dram_tensor("w1_r", [32, 128, 1024], F32, kind="ExternalInput").ap()
    w2_r = nc.dram_tensor("w2_r", [32, 128, 1024], F32, kind="ExternalInput").ap()
    colv_d = nc.dram_tensor("colv", [128, NCOL], F32, kind="ExternalInput").ap()
    gbc_d = nc.dram_tensor("gbc", [4, 128, D], F32, kind="ExternalInput").ap()
    ident_d = nc.dram_tensor("ident", [128, 128], F32, kind="ExternalInput").ap()
    colw4_d = nc.dram_tensor("colw4", [128, 256], F32, kind="ExternalInput").ap()
    e4_d = nc.dram_tensor("e4", [128, 32], F32, kind="ExternalInput").ap()
    y = nc.dram_tensor("y", [NT, D], F32, kind="ExternalOutput").ap()
    if DEBUG:
        dbg1 = nc.dram_tensor("dbg1", [NT, D], F32, kind="ExternalOutput").ap()
        dbg2 = nc.dram_tensor("dbg2", [128, 16384], BF16, kind="ExternalOutput").ap()
        dbg3 = nc.dram_tensor("dbg3", [128, 4096], F32, kind="ExternalOutput").ap()
        dbg4 = nc.dram_tensor("dbg4", [128, 1024], F32, kind="ExternalOutput").ap()

    P = Prog()
    with ExitStack() as es:
        def sb(name, shape, dt):
            return es.enter_context(nc.sbuf_tensor("sb_" + name, shape, dt))

        def psum(name):
            return es.enter_context(nc.psum_tensor(name, [128, 512], F32))

        def sem(name):
            return es.enter_context(nc.semaphore(name))

        R1 = sb("R1", [128, 33024], BF16)
        R2 = sb("R2", [128, 16384], F32)
        R3 = sb("R3", [128, 16384], BF16)
        SC1 = sb("SC1", [128, 4096], F32)
        SC2 = sb("SC2", [128, 8192], BF16)
        WS = [sb(f"WS{i}", [128, 1024], BF16) for i in range(4)]
        GBC = sb("GBC", [128, 1024], F32)
        JUNK = sb("JUNK", [128, 1024], BF16)
        GP2 = sb("GP2", [128, 1024], F32)
        identf = sb("identf", [128, 128], F32)
        identb = sb("identb", [128, 128], BF16)
        onesb = sb("onesb", [128, 128], BF16)
        colv = sb("colv", [128, NCOL], F32)
        hmask = sb("hmask", [128, 32], F32)
        colw4 = sb("colw4", [128, 256], F32)
        e4 = sb("e4", [128, 32], F32)
        stA = sb("stA", [128, 3 * 17], F32)
        st3 = sb("st3", [128, 16 * 5], F32)
        sth = sb("sth", [128, 16 * 3], F32)
        sto = sb("sto", [128, 16 * 5], F32)

        R1f = R1.bitcast(F32)
        R2b = R2.bitcast(BF16)
        R3f = R3.bitcast(F32)
        SC1b = SC1.bitcast(BF16)
        SC2f = SC2.bitcast(F32)

        hT = R1[:, 0:8 * NE].rearrange("p (k t) -> p k t", k=8)
        pA = R1[:, 8 * NE:8 * NE + 8 * NT].rearrange("p (k t) -> p k t", k=8)
        x1 = R1f[:, 0:16 * D].rearrange("p (t d) -> p t d", t=16)
        vv = R2b[:, 16384:32768].rearrange("p (k t) -> p k t", k=8)
        xstage = R2[:, :].rearrange("p (a d) -> p a d", a=16)
        mu_bf = GBC.bitcast(BF16)
        JUNKf = JUNK.bitcast(F32)
        mT = R2b[:, 0:8 * NT].rearrange("p (k t) -> p k t", k=8)
        w_o = R3[:, 0:8192].rearrange("p (k n) -> p k n", k=8)
        gp1 = R3f[:, 4096:5120]
        gp2 = GP2[:, :]
        fT = R2b[:, :].rearrange("p (m t) -> p m t", m=32)
        U4 = R3[:, 0:4 * NE].rearrange("p (g t) -> p g t", g=4)
        W4 = [R3[:, 12480 + i * 1024:12480 + (i + 1) * 1024].rearrange("p (q c) -> p q c", q=32) for i in range(2)]
        uBs = [R3[:, 8320 + i * NE:8320 + (i + 1) * NE] for i in range(2)]
        diagA = [R3[:, 14528 + i * 384:14528 + (i + 1) * 384].rearrange("p (t n) -> p t n", t=3) for i in range(2)]
        uAs = uBs
        vB = vv
        msq_tmp = R2[:, 0:2048]
        o2T = R3f[:, :].rearrange("p (i t) -> p i t", i=8)
        acc1 = SC1[:, 0:2048]
        acc2 = SC1[:, 2048:4096]
        h2T = SC1b[:, :].rearrange("p (k t) -> p k t", k=8)

        def qf(q, n=1):
            return SC2f[:, 512 * q:512 * (q + n)]

        def qb(q, n=1):
            return SC2[:, 1024 * q:1024 * (q + n)]

        PS = [psum(f"ps{i}") for i in range(8)]
        PSb = [p.bitcast(BF16) for p in PS]
        b_ps = [Buf(f"ps{i}") for i in range(8)]
        PP = [0, 1, 2, 3]
        PC = [4, 5]
        PS1, PS2 = 6, 7

        esems = {e: sem("s_" + e) for e in ("pe", "act", "dve", "pool")}
        s_c = sem("s_c")
        s_x = [sem(f"s_x{i}") for i in range(2)]
        s_w = [sem(f"s_w{i}") for i in range(4)]
        s_wo = sem("s_wo")
        s_u4 = [sem(f"s_u4_{g}") for g in range(4)]
        s_xt = [None] + [sem(f"s_xt{i}") for i in range(1, 4)]
        s_id = sem("s_id")
        s_xa = [sem(f"s_xa{i}") for i in range(4)]
        s_g = sem("s_g")
        s_gb = sem("s_gb")
        s_o = [sem(f"s_o{i}") for i in range(3)]
        block = es.enter_context(nc.Block())

        b_const = Buf("const")
        b_idb = Buf("identb")
        b_gbc = Buf("gbc")
        b_xa = [Buf(f"xa{i}") for i in range(4)]
        b_xa0 = Buf("xa_t0")
        b_xt = [None] + [Buf(f"xa_t{i}") for i in range(1, 4)]
        b_ident = Buf("identf")
        b_gp2 = Buf("gp2")
        b_jk = [Buf("jk0"), Buf("jk1")]
        b_sq = [Buf(f"sq{i}") for i in range(4)]
        b_q = [Buf(f"q{i}") for i in range(8)]
        b_ws = [Buf(f"ws{i}") for i in range(4)]
        b_hT = [Buf(f"hT{c}") for c in range(5)]
        b_pA = [Buf(f"pA{j}") for j in range(8)]
        b_v = [[Buf(f"v{j}_{n}") for n in range(4)] for j in range(8)]
        b_uB = [Buf("uB0"), Buf("uB1")]
        b_uA = b_uB
        b_dg = [Buf(f"W4_{i}") for i in range(2)]
        b_U4 = [Buf(f"U4_{g}") for g in range(4)]
        b_dgA = [Buf(f"diagA{i}") for i in range(2)]
        b_acc = [Buf(f"acc{n}") for n in range(4)]
        b_vB = [Buf(f"vB{n}") for n in range(4)]
        b_mT = Buf("mT")
        b_wo = Buf("wo")
        b_gp = Buf("gp12")
        b_x1 = [Buf(f"x1_{t}") for t in range(16)]
        b_h2T = Buf("h2T")
        b_fT_lo = Buf("fT_lo")
        b_fT_hi = Buf("fT_hi")
        b_o2T = Buf("o2T")
        b_stA = [Buf(f"stA{i}") for i in range(17)]
        b_st3 = [Buf(f"st3_{i}") for i in range(16)]
        b_sth = [Buf(f"sth{i}") for i in range(16)]
        b_sto = [Buf(f"sto{i}") for i in range(16)]

        wcount = [0]

        def wload(src_ap):
            s = wcount[0] % 4
            wcount[0] += 1
            P.dma("pool", lambda e, s=s, src_ap=src_ap: e.dma_start(out=WS[s][:], in_=src_ap), s_w[s], writes=[b_ws[s]])
            return s, WS[s][:, :].rearrange("p (k n) -> p k n", k=8)

        ppc = [0]

        def next_pp():
            i = PP[ppc[0] % 4]
            ppc[0] += 1
            return i

        pcc = [0]

        def next_pc():
            i = PC[pcc[0] % 2]
            pcc[0] += 1
            return i

        def col(c, n=1):
            return colv[:, c:c + n]

        for a in range(4):
            P.dma("sp", lambda e, a=a: e.dma_start(out=xstage[:, a, :], in_=x_ext[128 * a:128 * a + 128, :]),
                  s_g if a == 0 else s_xt[a], reads=([] if a == 0 else [b_xa0 if a == 1 else b_xt[a - 1]]),
                  writes=[b_xa0 if a == 0 else b_xt[a]])
            if a == 0:
                P.dma("sp", lambda e: e.dma_start(out=GBC[:], in_=gbc_d[0]), s_gb, writes=[b_gbc])
                P.dma("sp", lambda e: e.dma_start(out=identf[:], in_=ident_d), s_id, writes=[b_ident])
        for g4 in range(1, 4):
            P.dma("sp", lambda e, g4=g4: e.dma_start(out=xstage[:, 4 * g4:4 * g4 + 4, :],
                                                     in_=x_ext[512 * g4:512 * g4 + 512, :].rearrange("(a p) d -> p a d", p=128)),
                  s_xa[g4], writes=[b_xa[g4]])
        P.dma("sp", lambda e: e.dma_start(out=qf(0, 2)[0:32, :], in_=x_ext[2048:2080, :]), s_x[0], writes=[b_q[0], b_q[1]])
        P.dma("sp", lambda e: e.dma_start(out=colv[:], in_=colv_d), s_c, reads=[b_ident], writes=[b_const])
        P.dma("sp", lambda e: e.dma_start(out=hmask[:], in_=hmask_d), s_c, writes=[b_const])
        P.dma("sp", lambda e: e.dma_start(out=colw4[:], in_=colw4_d), s_c, writes=[b_const])
        P.dma("sp", lambda e: e.dma_start(out=e4[:], in_=e4_d), s_c, writes=[b_const])
        P.dma("sp", lambda e: e.dma_start(out=GP2[:], in_=gbc_d[2]), s_x[1], writes=[b_gp2])
        P.op("dve", lambda e: e.tensor_copy(out=identb[:], in_=identf[:]), reads=[b_ident], writes=[b_idb])

        def pa_tile(i):
            nr = 128 if i < 16 else 32
            if i < 16:
                return nr, xstage[:, i, :], [b_xa0 if i == 0 else (b_xt[i] if i < 4 else b_xa[i // 4])], qb(4 + i % 4), b_q[4 + i % 4]
            return nr, qf(0, 2), [b_q[0], b_q[1]], qb(4 + i % 4), b_q[4 + i % 4]

        def pa_stage1(i):
            nr, xin, bx, xn, bxn = pa_tile(i)
            P.op("act", lambda e: e.activation(out=JUNK[0:nr, :], in_=xin[0:nr, :], func=AF.Square, accum_out=stA[0:nr, i:i + 1]),
                 reads=bx, writes=[b_stA[i]])
            P.op("act", lambda e: e.activation(out=stA[0:nr, 17 + i:18 + i], in_=stA[0:nr, i:i + 1], func=AF.Sqrt, scale=1.0 / D, bias=RMS_EPS),
                 reads=[b_stA[i]], writes=[b_stA[i]])
            P.op("dve", lambda e: e.reciprocal(out=stA[0:nr, 34 + i:35 + i], in_=stA[0:nr, 17 + i:18 + i]),
                 reads=[b_stA[i]], writes=[b_stA[i]])
            P.op("dve", lambda e: e.scalar_tensor_tensor(out=xn[0:nr, :], in0=xin[0:nr, :], scalar=stA[0:nr, 34 + i:35 + i],
                                                         in1=GBC[0:nr, :], op0=ALU.mult, op1=ALU.mult),
                 reads=bx + [b_stA[i], b_gbc], writes=[bxn])

        pa_bank = {}

        def pa_stage2(i):
            nr, xin, bx, xn, bxn = pa_tile(i)
            pi = next_pc()
            pa_bank[i] = pi
            for k in range(8):
                P.op("pe", lambda e, k=k: e.transpose(out=PSb[pi][:, k * 128:k * 128 + nr], in_=xn[0:nr, k * 128:(k + 1) * 128],
                                                      identity=identb[0:nr, 0:nr]),
                     reads=[bxn, b_idb], writes=[b_ps[pi]])

        def pa_stage3(i):
            nr = 128 if i < 16 else 32
            r0 = 128 * i
            pi = pa_bank[i]
            src = PSb[pi][:, :].rearrange("p (k t) -> p k t", k=8)[:, :, 0:nr]
            if i >= 4 and i % 2 == 0:
                P.op("act", lambda e: e.activation(out=hT[:, :, r0:r0 + nr], in_=src, func=AF.Copy),
                     reads=[b_ps[pi]], writes=[b_hT[min(i // 4, 4)]])
            else:
                P.op("dve", lambda e: e.tensor_copy(out=hT[:, :, r0:r0 + nr], in_=src),
                     reads=[b_ps[pi]], writes=[b_hT[min(i // 4, 4)]])

        for step in range(17 + 2):
            if step < 17:
                pa_stage1(step)
            if 1 <= step < 18:
                pa_stage2(step - 1)
            if 2 <= step < 19:
                pa_stage3(step - 2)
        P.op("dve", lambda e: e.memset(onesb[:], 1.0), writes=[b_idb])
        P.op("dve", lambda e: e.memset(SC1[:], 0.0), writes=b_acc)
        Prog.alias([b for row in b_v for b in row] + [b_mT], b_xa[1:] + [b_xa0] + b_xt[1:])

        def hT_bufs(lo, hi):
            return [b_hT[c] for c in range(lo // 512, min((hi - 1) // 512, 4) + 1)]

        def inproj_group(wv, bws, lo, n):
            pi = next_pp()
            for k in range(8):
                P.op("pe", lambda e, pi=pi, k=k, wv=wv, lo=lo, n=n: e.matmul(PS[pi][:, 0:n], lhsT=wv[:, k, :], rhs=hT[:, k, lo:lo + n],
                                                                            start=(k == 0), stop=(k == 7)),
                     reads=[bws] + hT_bufs(lo, lo + n), writes=[b_ps[pi]])
            return pi

        pending_stats = []

        def emit_stats(items):
            for (j, n, sqt, vbt, bsq, bvb) in items:
                P.op("pe", lambda e, vbt=vbt: e.matmul(PS[PS1][:, :], lhsT=onesb[:], rhs=vbt, start=True, stop=True),
                     reads=[bvb, b_idb], writes=[b_ps[PS1]])
                P.op("pe", lambda e, sqt=sqt: e.matmul(PS[PS2][:, :], lhsT=onesb[:], rhs=sqt, start=True, stop=True),
                     reads=[bsq, b_idb], writes=[b_ps[PS2]])
                P.op("dve", lambda e, n=n: e.tensor_tensor(out=acc1[:, n * 512:(n + 1) * 512], in0=PS[PS1][:, :], in1=acc1[:, n * 512:(n + 1) * 512], op=ALU.add),
                     reads=[b_ps[PS1], b_acc[n]], writes=[b_acc[n]])
                P.op("dve", lambda e, n=n: e.tensor_tensor(out=acc2[:, n * 512:(n + 1) * 512], in0=PS[PS2][:, :], in1=acc2[:, n * 512:(n + 1) * 512], op=ALU.add),
                     reads=[b_ps[PS2], b_acc[n]], writes=[b_acc[n]])

        chunks = [(c * 512, 512) for c in range(4)] + [(2048, 32)]

        def pair(j, g_first, g_second, func, dst, bdst, qbase, after_chunk=None):
            s1, wv1 = wload(w_in_r[g_first * 8 + j])
            s2, wv2 = wload(w_in_r[g_second * 8 + j])
            for ci, (lo, n) in enumerate(chunks):
                if after_chunk is not None and ci > 0:
                    after_chunk(ci - 1)
                tq = qbase + ci % 2
                p1 = inproj_group(wv1, b_ws[s1], lo, n)
                P.op("act", lambda e, p1=p1, n=n, tq=tq: e.activation(out=qf(tq)[:, 0:n], in_=PS[p1][:, 0:n], func=func, bias=col(c_bin(g_first, j))),
                     reads=[b_ps[p1], b_const], writes=[b_q[tq]])
                p2 = inproj_group(wv2, b_ws[s2], lo, n)
                P.op("dve", lambda e, p2=p2, n=n, lo=lo, tq=tq: e.scalar_tensor_tensor(
                    out=dst[:, lo:lo + n], in0=PS[p2][:, 0:n], scalar=col(c_bin(g_second, j)), in1=qf(tq)[:, 0:n], op0=ALU.add, op1=ALU.mult),
                     reads=[b_ps[p2], b_q[tq], b_const], writes=[bdst])
            P.op("dve", lambda e: e.tensor_tensor(out=dst[:, 0:16], in0=dst[:, 0:16], in1=hmask[:, 0:16], op=ALU.mult),
                 reads=[bdst, b_const], writes=[bdst])
            P.op("dve", lambda e: e.tensor_tensor(out=dst[:, NE - 16:NE], in0=dst[:, NE - 16:NE], in1=hmask[:, 16:32], op=ALU.mult),
                 reads=[bdst, b_const], writes=[bdst])

        def build_diagB(j):
            jb = j % 2
            P.op("pool", lambda e: e.tensor_tensor(out=W4[jb], in0=e4[:, :].unsqueeze(1).to_broadcast([128, 32, 32]),
                                                   in1=colw4[:, j * 32:(j + 1) * 32].unsqueeze(2).to_broadcast([128, 32, 32]), op=ALU.mult),
                 reads=[b_const], writes=[b_dg[jb]])

        def u4_dma(j):
            ub_ = uBs[j % 2]
            for s_ in range(4):
                for g in range(4):
                    P.dma("sp", lambda e, g=g, s_=s_: e.dma_start(out=U4[32 * s_:32 * s_ + 32, g, 0:NE - 8 * s_], in_=ub_[32 * g:32 * g + 32, 8 * s_:NE]),
                          s_u4[g], reads=[b_uB[j % 2]], writes=[b_U4[g]])

        def build_diagA(j):
            jb = j % 2
            P.op("pool", lambda e: e.tensor_tensor(out=diagA[jb], in0=identb[:, :].unsqueeze(1).to_broadcast([128, 3, 128]),
                                                   in1=col(c_cwA(j), 3).unsqueeze(2).to_broadcast([128, 3, 128]), op=ALU.mult),
                 reads=[b_idb, b_const], writes=[b_dgA[jb]])

        def pairB(j, after_chunk=None):
            pair(j, 4, 3, AF.Sigmoid, uBs[j % 2], b_uB[j % 2], 2, after_chunk)

        def pairA(j, after_chunk=None):
            pair(j, 2, 1, AF.Identity, uAs[j % 2], b_uA[j % 2], 0, after_chunk)

        def conv31(j):
            jb = j % 2
            items = []
            for n in range(4):
                e0 = HALO + 512 * n
                pc = next_pc()
                for tap0 in range(8):
                    for g in range(4):
                        P.op("pe", lambda e, pc=pc, tap0=tap0, g=g, e0=e0: e.matmul(PS[pc][32 * g:32 * g + 32, :], lhsT=W4[jb][:, g * 8 + tap0, :],
                                                                                 rhs=U4[:, g, e0 + tap0 - 15:e0 + tap0 - 15 + 512],
                                                                                 start=(tap0 == 0), stop=(tap0 == 7), tile_position=(0, 32 * g)),
                             reads=[b_dg[jb], b_U4[g]], writes=[b_ps[pc]])
                vt = vv[:, j, n * 512:(n + 1) * 512]
                sqt = qb(6 + n // 2)[:, (n % 2) * 512:(n % 2) * 512 + 512]
                P.op("act", lambda e, pc=pc, vt=vt: e.activation(out=vt, in_=PS[pc][:, :], func=AF.Identity, bias=col(c_cbB(j))),
                     reads=[b_ps[pc], b_const], writes=[b_v[j][n]])
                P.op("act", lambda e, pc=pc, sqt=sqt: e.activation(out=sqt, in_=PS[pc][:, :], func=AF.Square, bias=col(c_cbB(j))),
                     reads=[b_ps[pc], b_const], writes=[b_sq[n]])
                items.append((j, n, sqt, vt, b_sq[n], b_v[j][n]))
            return items

        def conv3_bg(j, after_chunk=None):
            jb = j % 2
            uA = uAs[jb]
            s0, wv0 = wload(w_in_r[0 * 8 + j])
            for n in range(4):
                e0 = HALO + 512 * n
                pc = next_pc()
                for tap in range(3):
                    P.op("pe", lambda e, pc=pc, tap=tap, e0=e0: e.matmul(PS[pc][:, :], lhsT=diagA[jb][:, tap, :], rhs=uA[:, e0 + tap - 1:e0 + tap - 1 + 512],
                                                                      start=(tap == 0), stop=(tap == 2)),
                         reads=[b_dgA[jb], b_uA[jb]], writes=[b_ps[pc]])
                tq = 4 + n % 2
                P.op("act", lambda e, pc=pc, tq=tq: e.activation(out=qf(tq), in_=PS[pc][:, :], func=AF.Identity, bias=col(c_cbA(j))),
                     reads=[b_ps[pc], b_const], writes=[b_q[tq]])
                p0 = inproj_group(wv0, b_ws[s0], e0, 512)
                P.op("dve", lambda e, p0=p0, tq=tq, n=n: e.scalar_tensor_tensor(out=pA[:, j, n * 512:(n + 1) * 512], in0=PS[p0][:, :], scalar=col(c_bin(0, j)),
                                                                             in1=qf(tq), op0=ALU.add, op1=ALU.mult),
                     reads=[b_ps[p0], b_q[tq], b_const], writes=[b_pA[j]])
                if after_chunk is not None:
                    after_chunk()

        pairB(0)
        build_diagB(0)
        u4_dma(0)
        prev_items = None
        for j in range(8):
            if j + 1 < 8:
                pi_ = prev_items
                pairB(j + 1, after_chunk=(lambda ci, pi_=pi_: emit_stats([pi_[ci]])) if pi_ else None)
                build_diagB(j + 1)
            else:
                pi_ = prev_items
                pairA(0, after_chunk=lambda ci, pi_=pi_: emit_stats([pi_[ci]]))
                build_diagA(0)
            prev_items = conv31(j)
            if j + 1 < 8:
                u4_dma(j + 1)
        last_items = prev_items

        bg = []

        def fin_a(n, step):
            a1 = acc1[:, n * 512:(n + 1) * 512]
            a2 = acc2[:, n * 512:(n + 1) * 512]
            mq = msq_tmp[:, n * 512:(n + 1) * 512]
            if step == 0:
                P.op("dve", lambda e: e.tensor_scalar(out=a1, in0=a1, scalar1=1.0 / D, scalar2=None, op0=ALU.mult), reads=[b_acc[n]], writes=[b_acc[n]])
            elif step == 1:
                P.op("dve", lambda e: e.tensor_tensor(out=mq, in0=a1, in1=a1, op=ALU.mult), reads=[b_acc[n]], writes=[b_mT])
            elif step == 2:
                P.op("dve", lambda e: e.scalar_tensor_tensor(out=a2, in0=a2, scalar=1.0 / D, in1=mq, op0=ALU.mult, op1=ALU.subtract),
                     reads=[b_acc[n], b_mT], writes=[b_acc[n]])
            else:
                P.op("act", lambda e: e.activation(out=a2, in_=a2, func=AF.Sqrt, bias=LN_EPS), reads=[b_acc[n]], writes=[b_acc[n]])

        def fin_n(n):
            a1 = acc1[:, n * 512:(n + 1) * 512]
            a2 = acc2[:, n * 512:(n + 1) * 512]
            mb = mu_bf[:, n * 512:(n + 1) * 512]
            rb = SC1b[:, n * 1024:n * 1024 + 512]
            P.op("dve", lambda e: e.tensor_copy(out=mb, in_=a1), reads=[b_acc[n]], writes=[b_gbc])
            P.op("dve", lambda e: e.tensor_copy(out=rb, in_=a2), reads=[b_acc[n], b_gbc], writes=[b_acc[n]])

        def ln_unit(n, j, part):
            mb = mu_bf[:, n * 512:(n + 1) * 512]
            rb = SC1b[:, n * 1024:n * 1024 + 512]
            tq = j % 2
            tt_ = JUNK[:, tq * 512:(tq + 1) * 512]
            vt = vv[:, j, n * 512:(n + 1) * 512]
            if part == 0:
                P.op("dve", lambda e: e.tensor_tensor(out=tt_, in0=vt, in1=mb, op=ALU.subtract), reads=[b_v[j][n], b_gbc], writes=[b_jk[tq]])
                P.op("dve", lambda e: e.tensor_tensor(out=tt_, in0=tt_, in1=rb, op=ALU.mult), reads=[b_jk[tq], b_acc[n]], writes=[b_jk[tq]])
            else:
                P.op("act", lambda e: e.activation(out=vt, in_=tt_, func=AF.Silu, scale=col(c_lng(j)), bias=col(c_lnb(j))),
                     reads=[b_jk[tq], b_const], writes=[b_v[j][n]])

        def recip_half(n, hh):
            a2h = acc2[:, n * 512 + hh * 256:n * 512 + (hh + 1) * 256]
            P.op("dve", lambda e: e.reciprocal(out=a2h, in_=a2h), reads=[b_acc[n]], writes=[b_acc[n]])

        for n in range(4):
            for step in range(4):
                bg.append(lambda n=n, step=step: fin_a(n, step))
            bg.append(lambda n=n: recip_half(n, 0))
            bg.append(lambda n=n: recip_half(n, 1))
            bg.append(lambda n=n: fin_n(n))
            for j in range(9):
                if j < 8:
                    bg.append(lambda n=n, j=j: ln_unit(n, j, 0))
                if j >= 1:
                    bg.append(lambda n=n, j=j: ln_unit(n, j - 1, 1))

        def drain(k):
            for _ in range(k):
                if bg:
                    bg.pop(0)()

        for j in range(8):
            if j + 1 < 8:
                if j == 0:
                    pairA(1, after_chunk=lambda ci: emit_stats([last_items[ci]]))
                else:
                    pairA(j + 1)
                build_diagA(j + 1)
            if j == 0:
                drain(8)
            conv3_bg(j, after_chunk=lambda: drain(3))
        drain(len(bg))

        Prog.alias([b_q[6], b_q[7]], b_sq)
        allv = [b for row in b_v for b in row]
        wo_pending = []
        oldB = b_uB + b_dg + b_dgA + b_U4
        for i in range(8):
            if i == 1 and DEBUG:
                s_d2 = sem("s_dbg2")
                P.dma("sp", lambda e: e.dma_start(out=dbg2, in_=R3[:, :]), s_d2, reads=allv)
                P.dma("sp", lambda e: e.dma_start(out=dbg3, in_=SC1[:, :]), s_d2, reads=allv + b_acc)
                P.dma("sp", lambda e: e.dma_start(out=dbg4, in_=GBC[:, :]), s_d2, reads=allv + [b_gbc])
            if i == 2:
                Prog.alias([b_wo, b_gp], oldB)
                wo_pending.extend(range(8))
                P.dma("sp", lambda e: e.dma_start(out=gp1, in_=gbc_d[1]), s_g, writes=[b_gp])
            for (zg, wsrc, act_src, bsrc_of, first) in ((5, wa_r, pA, lambda n: b_pA, True), (6, wb_r, vB, lambda n: [b_v[jj][n] for jj in range(8)], False)):
                sz, wvz = wload(w_in_r[zg * 8 + i])
                sy, wvy = wload(wsrc[i])
                for _ in range(2):
                    if wo_pending:
                        k_ = wo_pending.pop(0)
                        P.dma("pool", lambda e, k=k_: e.dma_start(out=w_o[:, k, :], in_=w_o_d[k * 128:(k + 1) * 128, :]), s_wo, writes=[b_wo])
                for n in range(4):
                    e0 = HALO + 512 * n
                    sq_ = n % 2
                    pz = inproj_group(wvz, b_ws[sz], e0, 512)
                    P.op("act", lambda e, pz=pz, sq_=sq_, zg=zg, i=i: e.activation(out=qf(sq_), in_=PS[pz][:, :], func=AF.Sigmoid, bias=col(c_bin(zg, i))),
                         reads=[b_ps[pz], b_const], writes=[b_q[sq_]])
                    py = next_pp()
                    for k in range(8):
                        P.op("pe", lambda e, py=py, k=k, wvy=wvy, act_src=act_src, n=n: e.matmul(PS[py][:, :], lhsT=wvy[:, k, :], rhs=act_src[:, k, n * 512:(n + 1) * 512],
                                                                                           start=(k == 0), stop=(k == 7)),
                             reads=[b_ws[sy]] + bsrc_of(n), writes=[b_ps[py]])
                    if first:
                        P.op("dve", lambda e, py=py, sq_=sq_, n=n: e.tensor_tensor(out=qf(2 + n), in0=PS[py][:, :], in1=qf(sq_), op=ALU.mult),
                             reads=[b_ps[py], b_q[sq_]], writes=[b_q[2 + n]])
                    else:
                        P.op("dve", lambda e, py=py, sq_=sq_, n=n: e.tensor_tensor(out=qf(6 + sq_), in0=PS[py][:, :], in1=qf(sq_), op=ALU.mult),
                             reads=[b_ps[py], b_q[sq_]], writes=[b_q[6 + sq_]])
                        P.op("dve", lambda e, sq_=sq_, n=n, i=i: e.tensor_tensor(out=mT[:, i, n * 512:(n + 1) * 512], in0=qf(6 + sq_), in1=qf(2 + n), op=ALU.add),
                             reads=[b_q[6 + sq_], b_q[2 + n]], writes=[b_mT])

        Prog.alias(b_x1, b_hT + b_pA)
        Prog.alias([b_h2T], b_acc)

        def H2_elem(tt):
            hq = 6 + tt % 2
            x1t = x1[:, tt, :]
            P.op("act", lambda e, x1t=x1t, tt=tt: e.activation(out=JUNK[:, :], in_=x1t, func=AF.Square, accum_out=sth[:, tt:tt + 1]),
                 reads=[b_x1[tt]], writes=[b_sth[tt]])
            P.op("act", lambda e, tt=tt: e.activation(out=sth[:, 16 + tt:17 + tt], in_=sth[:, tt:tt + 1], func=AF.Sqrt, scale=1.0 / D, bias=RMS_EPS),
                 reads=[b_sth[tt]], writes=[b_sth[tt]])
            P.op("dve", lambda e, tt=tt: e.reciprocal(out=sth[:, 32 + tt:33 + tt], in_=sth[:, 16 + tt:17 + tt]), reads=[b_sth[tt]], writes=[b_sth[tt]])
            P.op("dve", lambda e, x1t=x1t, tt=tt, hq=hq: e.scalar_tensor_tensor(out=qb(hq), in0=x1t, scalar=sth[:, 32 + tt:33 + tt], in1=gp2, op0=ALU.mult, op1=ALU.mult),
                 reads=[b_x1[tt], b_sth[tt], b_gp2], writes=[b_q[hq]])

        def H2_tr(tt):
            ttl = tt % 8
            hq = 6 + tt % 2
            pi = next_pc()
            for k in range(8):
                P.op("pe", lambda e, pi=pi, k=k, hq=hq: e.transpose(out=PSb[pi][:, k * 128:(k + 1) * 128], in_=qb(hq)[:, k * 128:(k + 1) * 128], identity=identb[:]),
                     reads=[b_q[hq], b_idb], writes=[b_ps[pi]])
            P.op("act", lambda e, pi=pi, ttl=ttl: e.activation(out=h2T[:, :, ttl * 128:(ttl + 1) * 128], in_=PSb[pi][:, :].rearrange("p (k t) -> p k t", k=8), func=AF.Copy),
                 reads=[b_ps[pi]], writes=[b_h2T])

        def s3_step(tt):
            s = tt % 2
            xin = qf(2 * s, 2)
            bx = [b_q[2 * s], b_q[2 * s + 1]]
            P.dma("sp", lambda e, xin=xin, tt=tt: e.dma_start(out=xin, in_=x_ext[HALO + tt * 128:HALO + (tt + 1) * 128, :]), s_x[s], writes=bx)
            pis = []
            for nh in range(2):
                pi = next_pp()
                pis.append(pi)
                for k in range(8):
                    P.op("pe", lambda e, pi=pi, k=k, tt=tt, nh=nh: e.matmul(PS[pi][:, :], lhsT=mT[:, k, tt * 128:(tt + 1) * 128], rhs=w_o[:, k, nh * 512:(nh + 1) * 512],
                                                                         start=(k == 0), stop=(k == 7)),
                         reads=[b_mT, b_wo], writes=[b_ps[pi]])
                P.op("act", lambda e, pi=pi, tt=tt, nh=nh: e.activation(out=JUNK[:, 0:512], in_=PS[pi][:, :], func=AF.Square, accum_out=st3[:, 2 * tt + nh:2 * tt + nh + 1]),
                     reads=[b_ps[pi]], writes=[b_st3[tt]])
            P.op("dve", lambda e, tt=tt: e.tensor_tensor(out=st3[:, 32 + tt:33 + tt], in0=st3[:, 2 * tt:2 * tt + 1], in1=st3[:, 2 * tt + 1:2 * tt + 2], op=ALU.add),
                 reads=[b_st3[tt]], writes=[b_st3[tt]])
            P.op("act", lambda e, tt=tt: e.activation(out=st3[:, 48 + tt:49 + tt], in_=st3[:, 32 + tt:33 + tt], func=AF.Sqrt, scale=1.0 / D, bias=RMS_EPS),
                 reads=[b_st3[tt]], writes=[b_st3[tt]])
            P.op("dve", lambda e, tt=tt: e.reciprocal(out=st3[:, 64 + tt:65 + tt], in_=st3[:, 48 + tt:49 + tt]), reads=[b_st3[tt]], writes=[b_st3[tt]])
            for nh in range(2):
                P.op("dve", lambda e, pi=pis[nh], tt=tt, nh=nh: e.scalar_tensor_tensor(out=x1[:, tt, nh * 512:(nh + 1) * 512], in0=PS[pi][:, :], scalar=st3[:, 64 + tt:65 + tt],
                                                                                in1=gp1[:, nh * 512:(nh + 1) * 512], op0=ALU.mult, op1=ALU.mult),
                     reads=[b_ps[pis[nh]], b_st3[tt], b_gp], writes=[b_x1[tt]])
            P.op("dve", lambda e, tt=tt, xin=xin: e.tensor_tensor(out=x1[:, tt, :], in0=x1[:, tt, :], in1=xin, op=ALU.add),
                 reads=[b_x1[tt]] + bx, writes=[b_x1[tt]])
            if 1 <= tt <= 8:
                H2_elem(tt - 1)
            if 2 <= tt <= 9:
                H2_tr(tt - 2)


        for tt in range(10):
            s3_step(tt)

        if DEBUG:
            s_d = sem("s_dbg")
            for tt in range(16):
                P.dma("sp", lambda e, tt=tt: e.dma_start(out=dbg1[tt * 128:(tt + 1) * 128, :], in_=x1[:, tt, :]), s_d, reads=[b_x1[tt]])
        P.dma("sp", lambda e: e.dma_start(out=GBC[:], in_=gbc_d[3]), s_gb, writes=[b_gbc])

        def mlp_in_block(h, m):
            s, wv = wload(w1_r[m])
            for n in range(2):
                pi = next_pp()
                for k in range(8):
                    P.op("pe", lambda e, pi=pi, k=k, wv=wv, n=n: e.matmul(PS[pi][:, :], lhsT=wv[:, k, :], rhs=h2T[:, k, n * 512:(n + 1) * 512], start=(k == 0), stop=(k == 7)),
                         reads=[b_ws[s], b_h2T], writes=[b_ps[pi]])
                rq = 6 + n
                P.op("act", lambda e, pi=pi, rq=rq: e.activation(out=qf(rq), in_=PS[pi][:, :], func=AF.Relu), reads=[b_ps[pi]], writes=[b_q[rq]])
                P.op("dve", lambda e, rq=rq, m=m, n=n: e.tensor_tensor(out=fT[:, m, n * 512:(n + 1) * 512], in0=qf(rq), in1=qf(rq), op=ALU.mult),
                     reads=[b_q[rq]], writes=[b_fT_hi if m >= 16 else b_fT_lo])

        def mlp_out_mm(h, hook=None, early_tail=None):
            for i in range(8):
                pis = [PP[(i % 2) * 2 + n] for n in range(2)]
                if early_tail is not None and i == 7:
                    pieces = [wload(w2_r[i * 4 + kq]) for kq in range(4)]
                    for n in range(2):
                        for kq in range(4):
                            s, wv = pieces[kq]
                            for kk in range(8):
                                P.op("pe", lambda e, pi=pis[n], wv=wv, kk=kk, kq=kq, n=n: e.matmul(PS[pi][:, :], lhsT=wv[:, kk, :], rhs=fT[:, kq * 8 + kk, n * 512:(n + 1) * 512],
                                                                                               start=(kq == 0 and kk == 0), stop=(kq == 3 and kk == 7)),
                                     reads=[b_ws[s], b_fT_lo, b_fT_hi], writes=[b_ps[pis[n]]])
                            if n == 1:
                                early_tail(kq)
                        if n == 0:
                            P.op("act", lambda e, pi=pis[0], i=i: e.activation(out=o2T[:, i, 0:512], in_=PS[pi][:, :], func=AF.Copy),
                                 reads=[b_ps[pis[0]]], writes=[b_o2T])
                    P.op("act", lambda e, pi=pis[1], i=i: e.activation(out=o2T[:, i, 512:1024], in_=PS[pi][:, :], func=AF.Copy),
                         reads=[b_ps[pis[1]]], writes=[b_o2T])
                    continue
                for kq in range(4):
                    s, wv = wload(w2_r[i * 4 + kq])
                    for n in range(2):
                        for kk in range(8):
                            P.op("pe", lambda e, pi=pis[n], wv=wv, kk=kk, kq=kq, n=n: e.matmul(PS[pi][:, :], lhsT=wv[:, kk, :], rhs=fT[:, kq * 8 + kk, n * 512:(n + 1) * 512],
                                                                                           start=(kq == 0 and kk == 0), stop=(kq == 3 and kk == 7)),
                                 reads=[b_ws[s], b_fT_lo, b_fT_hi], writes=[b_ps[pis[n]]])
                for n in range(2):
                    P.op("act", lambda e, pi=pis[n], i=i, n=n: e.activation(out=o2T[:, i, n * 512:(n + 1) * 512], in_=PS[pi][:, :], func=AF.Copy),
                         reads=[b_ps[pis[n]]], writes=[b_o2T])
                if hook is not None:
                    hook(i)

        def mlp_out_tail(h, ttl):
            tt = h * 8 + ttl
            banks = [PC[0], PC[1]] if ttl % 2 == 0 else [PS1, PS2]
            for i in range(8):
                pi = banks[i // 4]
                P.op("pe", lambda e, pi=pi, i=i, ttl=ttl: e.transpose(out=PS[pi][:, (i % 4) * 128:(i % 4 + 1) * 128], in_=o2T[:, i, ttl * 128:(ttl + 1) * 128], identity=identf[:]),
                     reads=[b_o2T, b_ident], writes=[b_ps[pi]])
            for x in range(2):
                P.op("act", lambda e, pi=banks[x], tt=tt, x=x: e.activation(out=JUNK[:, 0:512], in_=PS[pi][:, :], func=AF.Square, accum_out=sto[:, 2 * tt + x:2 * tt + x + 1]),
                     reads=[b_ps[banks[x]]], writes=[b_sto[tt]])
            P.op("dve", lambda e, tt=tt: e.tensor_tensor(out=sto[:, 32 + tt:33 + tt], in0=sto[:, 2 * tt:2 * tt + 1], in1=sto[:, 2 * tt + 1:2 * tt + 2], op=ALU.add),
                 reads=[b_sto[tt]], writes=[b_sto[tt]])
            P.op("act", lambda e, tt=tt: e.activation(out=sto[:, 48 + tt:49 + tt], in_=sto[:, 32 + tt:33 + tt], func=AF.Sqrt, scale=1.0 / D, bias=RMS_EPS),
                 reads=[b_sto[tt]], writes=[b_sto[tt]])
            P.op("dve", lambda e, tt=tt: e.reciprocal(out=sto[:, 64 + tt:65 + tt], in_=sto[:, 48 + tt:49 + tt]), reads=[b_sto[tt]], writes=[b_sto[tt]])
            s = tt % 3
            yout = qf(2 * s, 2)
            by = [b_q[2 * s], b_q[2 * s + 1]]
            for x in range(2):
                P.op("dve", lambda e, pi=banks[x], tt=tt, x=x, yout=yout: e.scalar_tensor_tensor(out=yout[:, x * 512:(x + 1) * 512], in0=PS[pi][:, :], scalar=sto[:, 64 + tt:65 + tt],
                                                                                          in1=GBC[:, x * 512:(x + 1) * 512], op0=ALU.mult, op1=ALU.mult),
                     reads=[b_ps[banks[x]], b_sto[tt], b_gbc], writes=by)
            P.op("pool" if h == 1 else "dve", lambda e, tt=tt, yout=yout: e.tensor_tensor(out=yout, in0=yout, in1=x1[:, tt, :], op=ALU.add),
                 reads=by + [b_x1[tt]], writes=by)
            P.dma("sp", lambda e, tt=tt, yout=yout: e.dma_start(out=y[tt * 128:(tt + 1) * 128, :], in_=yout), s_o[s], reads=by)

        Prog.alias([b_fT_hi], allv)
        hi_blocks = list(range(16, 32))
        for tt in range(10, 16):
            s3_step(tt)
            for _ in range(3):
                if hi_blocks:
                    mlp_in_block(0, hi_blocks.pop(0))
        Prog.alias([b_fT_lo], [b_mT])
        Prog.alias([b_o2T], [b_wo, b_gp])
        for m in hi_blocks + list(range(16)):
            mlp_in_block(0, m)
        def hook0(i):
            if 8 + i + 1 <= 15:
                H2_elem(8 + i + 1)
            H2_tr(8 + i)

        H2_elem(8)
        mlp_out_mm(0, hook=hook0)
        for m in range(32):
            mlp_in_block(1, m)
            if 4 <= m < 12:
                mlp_out_tail(0, m - 4)
        mlp_out_mm(1, early_tail=lambda kq: mlp_out_tail(1, kq))
        for ttl in range(4, 8):
            mlp_out_tail(1, ttl)

        for i in range(3):
            P.final_waits.append((s_o[i], P.dma_counts[id(s_o[i])]))
        if DEBUG:
            P.final_waits.append((s_d, P.dma_counts[id(s_d)]))
            P.final_waits.append((s_d2, P.dma_counts[id(s_d2)]))
        P.emit(block, esems)
    return nc


def _blocks(w, nblk):
    return np.ascontiguousarray(w.reshape(8, 128, nblk, 128).transpose(2, 1, 0, 3).reshape(nblk, 128, 1024))


def _colmajor(v):
    return v.reshape(-1, 128).T


_NC_CACHE = {}


def kernel(x, norm1_pre_g, w_in, b_in, conv_a_w, conv_a_b, w_a_out, conv_b_w, conv_b_b, ln_b_g, ln_b_b,
           w_b_out, w_o, norm1_post_g, norm2_pre_g, w_mlp_in, w_mlp_out, norm2_post_g):
    f = lambda a: np.asarray(a, dtype=np.float32)
    x = f(x)
    w_in = f(w_in)
    B, S, _ = x.shape
    w_in_r = np.ascontiguousarray(w_in.reshape(8, 128, 7, 8, 128).transpose(2, 3, 1, 0, 4).reshape(56, 128, 1024))
    wa_r = _blocks(f(w_a_out), 8)
    wb_r = _blocks(f(w_b_out), 8)
    w1_r = _blocks(f(w_mlp_in), 32)
    w2_r = np.ascontiguousarray(f(w_mlp_out).reshape(4, 8, 128, 8, 128).transpose(3, 0, 2, 1, 4).reshape(32, 128, 1024))
    colv = np.zeros((128, NCOL), np.float32)
    colv[:, 0:56] = _colmajor(f(b_in))
    colv[:, 56:64] = _colmajor(f(conv_a_b))
    colv[:, 64:72] = _colmajor(f(conv_b_b))
    colv[:, 72:80] = _colmajor(f(ln_b_g))
    colv[:, 80:88] = _colmajor(f(ln_b_b))
    cwa = f(conv_a_w)
    cwb = f(conv_b_w)
    colv[:, 88:112] = cwa.reshape(3, 8, 128).transpose(2, 1, 0).reshape(128, 24)
    cwb_pad = np.concatenate([cwb, np.zeros((1, D), np.float32)], axis=0)
    colw4 = np.ascontiguousarray(cwb_pad.reshape(4, 8, 8, 4, 32).transpose(0, 4, 2, 3, 1).reshape(128, 256))
    e4 = np.ascontiguousarray(np.tile(np.eye(32, dtype=np.float32), (4, 1)))
    gbc = np.ascontiguousarray(np.broadcast_to(
        np.stack([f(norm1_pre_g), f(norm1_post_g), f(norm2_pre_g), f(norm2_post_g)])[:, None, :], (4, 128, D)))
    ident = np.eye(128, dtype=np.float32)
    w_o = np.ascontiguousarray(f(w_o))

    in_maps = []
    nq = S // NT
    for c in range(8):
        b, q = c // nq, c % nq
        t0 = q * NT
        xe = np.zeros((NE, D), np.float32)
        lo, hi = max(t0 - HALO, 0), min(t0 + NT + HALO, S)
        xe[lo - (t0 - HALO):hi - (t0 - HALO)] = x[b, lo:hi]
        hm = np.zeros((128, 32), np.float32)
        if q > 0:
            hm[:, 0:16] = 1.0
        if q < nq - 1:
            hm[:, 16:32] = 1.0
        in_maps.append({"x_ext": xe, "hmask": hm, "w_in_r": w_in_r, "wa_r": wa_r, "wb_r": wb_r, "w_o": w_o,
                        "w1_r": w1_r, "w2_r": w2_r, "colv": colv, "gbc": gbc, "ident": ident, "colw4": colw4, "e4": e4})
    if "nc" not in _NC_CACHE:
        _NC_CACHE["nc"] = build_nc()
    nc = _NC_CACHE["nc"]
    res = run_bass_kernel_spmd(nc, in_maps, core_ids=list(range(8)))
    if DEBUG:
        _LAST["res"] = res
    out = np.empty((B, S, D), np.float32)
    for c in range(8):
        b, q = c // nq, c % nq
        out[b, q * NT:(q + 1) * NT] = res.results[c]["y"]
    return out
```

```python
from contextlib import ExitStack

import numpy as np
import concourse.bass as bass
import concourse.mybir as mybir
from concourse.bass_utils import run_bass_kernel_spmd

F32 = mybir.dt.float32
BF16 = mybir.dt.bfloat16
AF = mybir.ActivationFunctionType
ALU = mybir.AluOpType

D = 1024
NT = 2048
HALO = 16
NE = NT + 2 * HALO
DFF = 4096
RMS_EPS = 1e-6
LN_EPS = 1e-5
NCOL = 112
DEBUG = False
_LAST = {}


class Buf:
    __slots__ = ("name", "writers", "readers")

    def __init__(self, name):
        self.name = name
        self.writers = []
        self.readers = []


class Op:
    __slots__ = ("eng", "idx", "fn", "deps", "is_dma", "dsem", "dval", "needs_signal", "sigval")

    def __init__(self, eng, fn, is_dma=False, dsem=None):
        self.eng = eng
        self.fn = fn
        self.deps = []
        self.is_dma = is_dma
        self.dsem = dsem
        self.dval = 0
        self.needs_signal = False
        self.sigval = 0


class Prog:
    ENGS = ("pe", "act", "dve", "pool", "sp")

    def __init__(self):
        self.ops = {e: [] for e in self.ENGS}
        self.dma_counts = {}
        self.final_waits = []

    def _add(self, op, reads, writes):
        lst = self.ops[op.eng]
        op.idx = len(lst)
        lst.append(op)
        deps = []
        same = (lambda d: (not d.is_dma) and (not op.is_dma) and d.eng == op.eng)
        for b in reads:
            deps.extend(b.writers)
        for b in writes:
            deps.extend(d for d in b.readers if not same(d))
            deps.extend(d for d in b.writers if not same(d))
        for b in reads:
            b.readers.append(op)
        for b in writes:
            if b.readers:
                b.writers = [op]
                b.readers = []
            else:
                b.writers.append(op)
        best = {}
        for d in deps:
            if d is op:
                continue
            if d.is_dma:
                key = ("dma", id(d.dsem))
                if key not in best or best[key].dval < d.dval:
                    best[key] = d
            else:
                if d.eng == "pe" and op.eng == "pe" and not op.is_dma:
                    continue
                key = ("eng", d.eng)
                if key not in best or best[key].idx < d.idx:
                    best[key] = d
        op.deps = list(best.values())
        for d in op.deps:
            d.needs_signal = True
        return op

    def op(self, eng, fn, reads=(), writes=()):
        return self._add(Op(eng, fn), list(reads), list(writes))

    def dma(self, queue, fn, sem, reads=(), writes=()):
        o = Op(queue, fn, is_dma=True, dsem=sem)
        c = self.dma_counts.get(id(sem), 0) + 16
        self.dma_counts[id(sem)] = c
        o.dval = c
        return self._add(o, list(reads), list(writes))

    @staticmethod
    def alias(new_bufs, old_bufs):
        olds = []
        for ob in old_bufs:
            olds.extend(ob.readers)
            olds.extend(ob.writers)
        for nb in new_bufs:
            nb.readers.extend(olds)

    def emit(self, block, esems):
        for e in self.ENGS:
            cnt = 0
            for o in self.ops[e]:
                if (not o.is_dma) and o.needs_signal:
                    cnt += 1
                    o.sigval = cnt
        engobj = {"pe": "tensor", "act": "scalar", "dve": "vector", "pool": "gpsimd", "sp": "sync"}
        prog = self

        def make(e):
            def body(eng):
                seen = {}
                for o in prog.ops[e]:
                    waits = {}
                    for d in o.deps:
                        if d.is_dma:
                            sem, val = d.dsem, d.dval
                        else:
                            sem, val = esems[d.eng], d.sigval
                        k = id(sem)
                        if k not in waits or waits[k][1] < val:
                            waits[k] = (sem, val)
                    for k, (sem, val) in waits.items():
                        if seen.get(k, 0) < val:
                            eng.wait_ge(sem, val)
                            seen[k] = val
                    inst = o.fn(eng)
                    if o.is_dma:
                        inst.then_inc(o.dsem, 16)
                    elif o.needs_signal:
                        inst.then_inc(esems[e], 1)
                if e == "sp":
                    for sem, val in prog.final_waits:
                        eng.wait_ge(sem, val)
            return body

        for e in self.ENGS:
            if self.ops[e] or (e == "sp" and self.final_waits):
                getattr(block, engobj[e])(make(e))


def c_bin(grp, j):
    return grp * 8 + j


def c_cbA(j):
    return 56 + j


def c_cbB(j):
    return 64 + j


def c_lng(j):
    return 72 + j


def c_lnb(j):
    return 80 + j


def c_cwA(j, tap=0):
    return 88 + j * 3 + tap


def c_cwB(j, tap=0):
    return 112 + j * 31 + tap


def build_nc():
    nc = bass.Bass("TRN2", target_bir_lowering=False, dynamic_dma_scratch_size=8192)
    x_ext = nc.dram_tensor("x_ext", [NE, D], F32, kind="ExternalInput").ap()
    hmask_d = nc.dram_tensor("hmask", [128, 32], F32, kind="ExternalInput").ap()
    w_in_r = nc.dram_tensor("w_in_r", [56, 128, 1024], F32, kind="ExternalInput").ap()
    wa_r = nc.dram_tensor("wa_r", [8, 128, 1024], F32, kind="ExternalInput").ap()
    wb_r = nc.dram_tensor("wb_r", [8, 128, 1024], F32, kind="ExternalInput").ap()
    w_o_d = nc.dram_tensor("w_o", [D, D], F32, kind="ExternalInput").ap()
    w1_r = nc.dram_tensor("w1_r", [32, 128, 1024], F32, kind="ExternalInput").ap()
    w2_r = nc.dram_tensor("w2_r", [32, 128, 1024], F32, kind="ExternalInput").ap()
    colv_d = nc.dram_tensor("colv", [128, NCOL], F32, kind="ExternalInput").ap()
    gbc_d = nc.dram_tensor("gbc", [4, 128, D], F32, kind="ExternalInput").ap()
    ident_d = nc.dram_tensor("ident", [128, 128], F32, kind="ExternalInput").ap()
    colw4_d = nc.dram_tensor("colw4", [128, 256], F32, kind="ExternalInput").ap()
    e4_d = nc.dram_tensor("e4", [128, 32], F32, kind="ExternalInput").ap()
    y = nc.dram_tensor("y", [NT, D], F32, kind="ExternalOutput").ap()
    if DEBUG:
        dbg1 = nc.dram_tensor("dbg1", [NT, D], F32, kind="ExternalOutput").ap()
        dbg2 = nc.dram_tensor("dbg2", [128, 16384], BF16, kind="ExternalOutput").ap()
        dbg3 = nc.dram_tensor("dbg3", [128, 4096], F32, kind="ExternalOutput").ap()
        dbg4 = nc.dram_tensor("dbg4", [128, 1024], F32, kind="ExternalOutput").ap()

    P = Prog()
    with ExitStack() as es:
        def sb(name, shape, dt):
            return es.enter_context(nc.sbuf_tensor("sb_" + name, shape, dt))

        def psum(name):
            return es.enter_context(nc.psum_tensor(name, [128, 512], F32))

        def sem(name):
            return es.enter_context(nc.semaphore(name))

        R1 = sb("R1", [128, 33024], BF16)
        R2 = sb("R2", [128, 16384], F32)
        R3 = sb("R3", [128, 16384], BF16)
        SC1 = sb("SC1", [128, 4096], F32)
        SC2 = sb("SC2", [128, 8192], BF16)
        WS = [sb(f"WS{i}", [128, 1024], BF16) for i in range(4)]
        GBC = sb("GBC", [128, 1024], F32)
        JUNK = sb("JUNK", [128, 1024], BF16)
        GP2 = sb("GP2", [128, 1024], F32)
        identf = sb("identf", [128, 128], F32)
        identb = sb("identb", [128, 128], BF16)
        onesb = sb("onesb", [128, 128], BF16)
        colv = sb("colv", [128, NCOL], F32)
        hmask = sb("hmask", [128, 32], F32)
        colw4 = sb("colw4", [128, 256], F32)
        e4 = sb("e4", [128, 32], F32)
        stA = sb("stA", [128, 3 * 17], F32)
        st3 = sb("st3", [128, 16 * 5], F32)
        sth = sb("sth", [128, 16 * 3], F32)
        sto = sb("sto", [128, 16 * 5], F32)

        R1f = R1.bitcast(F32)
        R2b = R2.bitcast(BF16)
        R3f = R3.bitcast(F32)
        SC1b = SC1.bitcast(BF16)
        SC2f = SC2.bitcast(F32)

        hT = R1[:, 0:8 * NE].rearrange("p (k t) -> p k t", k=8)
        pA = R1[:, 8 * NE:8 * NE + 8 * NT].rearrange("p (k t) -> p k t", k=8)
        x1 = R1f[:, 0:16 * D].rearrange("p (t d) -> p t d", t=16)
        vv = R2b[:, 16384:32768].rearrange("p (k t) -> p k t", k=8)
        xstage = R2[:, :].rearrange("p (a d) -> p a d", a=16)
        mu_bf = GBC.bitcast(BF16)
        JUNKf = JUNK.bitcast(F32)
        mT = R2b[:, 0:8 * NT].rearrange("p (k t) -> p k t", k=8)
        w_o = R3[:, 0:8192].rearrange("p (k n) -> p k n", k=8)
        gp1 = R3f[:, 4096:5120]
        gp2 = GP2[:, :]
        fT = R2b[:, :].rearrange("p (m t) -> p m t", m=32)
        U4 = R3[:, 0:4 * NE].rearrange("p (g t) -> p g t", g=4)
        W4 = [R3[:, 12480 + i * 1024:12480 + (i + 1) * 1024].rearrange("p (q c) -> p q c", q=32) for i in range(2)]
        uBs = [R3[:, 8320 + i * NE:8320 + (i + 1) * NE] for i in range(2)]
        diagA = [R3[:, 14528 + i * 384:14528 + (i + 1) * 384].rearrange("p (t n) -> p t n", t=3) for i in range(2)]
        uAs = uBs
        vB = vv
        msq_tmp = R2[:, 0:2048]
        o2T = R3f[:, :].rearrange("p (i t) -> p i t", i=8)
        acc1 = SC1[:, 0:2048]
        acc2 = SC1[:, 2048:4096]
        h2T = SC1b[:, :].rearrange("p (k t) -> p k t", k=8)

        def qf(q, n=1):
            return SC2f[:, 512 * q:512 * (q + n)]

        def qb(q, n=1):
            return SC2[:, 1024 * q:1024 * (q + n)]

        PS = [psum(f"ps{i}") for i in range(8)]
        PSb = [p.bitcast(BF16) for p in PS]
        b_ps = [Buf(f"ps{i}") for i in range(8)]
        PP = [0, 1, 2, 3]
        PC = [4, 5]
        PS1, PS2 = 6, 7

        esems = {e: sem("s_" + e) for e in ("pe", "act", "dve", "pool")}
        s_c = sem("s_c")
        s_x = [sem(f"s_x{i}") for i in range(2)]
        s_w = [sem(f"s_w{i}") for i in range(4)]
        s_wo = sem("s_wo")
        s_u4 = [sem(f"s_u4_{g}") for g in range(4)]
        s_xt = [None] + [sem(f"s_xt{i}") for i in range(1, 4)]
        s_id = sem("s_id")
        s_xa = [sem(f"s_xa{i}") for i in range(4)]
        s_g = sem("s_g")
        s_gb = sem("s_gb")
        s_o = [sem(f"s_o{i}") for i in range(3)]
        block = es.enter_context(nc.Block())

        b_const = Buf("const")
        b_idb = Buf("identb")
        b_gbc = Buf("gbc")
        b_xa = [Buf(f"xa{i}") for i in range(4)]
        b_xa0 = Buf("xa_t0")
        b_xt = [None] + [Buf(f"xa_t{i}") for i in range(1, 4)]
        b_ident = Buf("identf")
        b_gp2 = Buf("gp2")
        b_jk = [Buf("jk0"), Buf("jk1")]
        b_sq = [Buf(f"sq{i}") for i in range(4)]
        b_q = [Buf(f"q{i}") for i in range(8)]
        b_ws = [Buf(f"ws{i}") for i in range(4)]
        b_hT = [Buf(f"hT{c}") for c in range(5)]
        b_pA = [Buf(f"pA{j}") for j in range(8)]
        b_v = [[Buf(f"v{j}_{n}") for n in range(4)] for j in range(8)]
        b_uB = [Buf("uB0"), Buf("uB1")]
        b_uA = b_uB
        b_dg = [Buf(f"W4_{i}") for i in range(2)]
        b_U4 = [Buf(f"U4_{g}") for g in range(4)]
        b_dgA = [Buf(f"diagA{i}") for i in range(2)]
        b_acc = [Buf(f"acc{n}") for n in range(4)]
        b_vB = [Buf(f"vB{n}") for n in range(4)]
        b_mT = Buf("mT")
        b_wo = Buf("wo")
        b_gp = Buf("gp12")
        b_x1 = [Buf(f"x1_{t}") for t in range(16)]
        b_h2T = Buf("h2T")
        b_fT_lo = Buf("fT_lo")
        b_fT_hi = Buf("fT_hi")
        b_o2T = Buf("o2T")
        b_stA = [Buf(f"stA{i}") for i in range(17)]
        b_st3 = [Buf(f"st3_{i}") for i in range(16)]
        b_sth = [Buf(f"sth{i}") for i in range(16)]
        b_sto = [Buf(f"sto{i}") for i in range(16)]

        wcount = [0]

        def wload(src_ap):
            s = wcount[0] % 4
            wcount[0] += 1
            P.dma("pool", lambda e, s=s, src_ap=src_ap: e.dma_start(out=WS[s][:], in_=src_ap), s_w[s], writes=[b_ws[s]])
            return s, WS[s][:, :].rearrange("p (k n) -> p k n", k=8)

        ppc = [0]

        def next_pp():
            i = PP[ppc[0] % 4]
            ppc[0] += 1
            return i

        pcc = [0]

        def next_pc():
            i = PC[pcc[0] % 2]
            pcc[0] += 1
            return i

        def col(c, n=1):
            return colv[:, c:c + n]

        for a in range(4):
            P.dma("sp", lambda e, a=a: e.dma_start(out=xstage[:, a, :], in_=x_ext[128 * a:128 * a + 128, :]),
                  s_g if a == 0 else s_xt[a], reads=([] if a == 0 else [b_xa0 if a == 1 else b_xt[a - 1]]),
                  writes=[b_xa0 if a == 0 else b_xt[a]])
            if a == 0:
                P.dma("sp", lambda e: e.dma_start(out=GBC[:], in_=gbc_d[0]), s_gb, writes=[b_gbc])
                P.dma("sp", lambda e: e.dma_start(out=identf[:], in_=ident_d), s_id, writes=[b_ident])
        for g4 in range(1, 4):
            P.dma("sp", lambda e, g4=g4: e.dma_start(out=xstage[:, 4 * g4:4 * g4 + 4, :],
                                                     in_=x_ext[512 * g4:512 * g4 + 512, :].rearrange("(a p) d -> p a d", p=128)),
                  s_xa[g4], writes=[b_xa[g4]])
        P.dma("sp", lambda e: e.dma_start(out=qf(0, 2)[0:32, :], in_=x_ext[2048:2080, :]), s_x[0], writes=[b_q[0], b_q[1]])
        P.dma("sp", lambda e: e.dma_start(out=colv[:], in_=colv_d), s_c, reads=[b_ident], writes=[b_const])
        P.dma("sp", lambda e: e.dma_start(out=hmask[:], in_=hmask_d), s_c, writes=[b_const])
        P.dma("sp", lambda e: e.dma_start(out=colw4[:], in_=colw4_d), s_c, writes=[b_const])
        P.dma("sp", lambda e: e.dma_start(out=e4[:], in_=e4_d), s_c, writes=[b_const])
        P.dma("sp", lambda e: e.dma_start(out=GP2[:], in_=gbc_d[2]), s_x[1], writes=[b_gp2])
        P.op("dve", lambda e: e.tensor_copy(out=identb[:], in_=identf[:]), reads=[b_ident], writes=[b_idb])

        def pa_tile(i):
            nr = 128 if i < 16 else 32
            if i < 16:
                return nr, xstage[:, i, :], [b_xa0 if i == 0 else (b_xt[i] if i < 4 else b_xa[i // 4])], qb(4 + i % 4), b_q[4 + i % 4]
            return nr, qf(0, 2), [b_q[0], b_q[1]], qb(4 + i % 4), b_q[4 + i % 4]

        def pa_stage1(i):
            nr, xin, bx, xn, bxn = pa_tile(i)
            P.op("act", lambda e: e.activation(out=JUNK[0:nr, :], in_=xin[0:nr, :], func=AF.Square, accum_out=stA[0:nr, i:i + 1]),
                 reads=bx, writes=[b_stA[i]])
            P.op("act", lambda e: e.activation(out=stA[0:nr, 17 + i:18 + i], in_=stA[0:nr, i:i + 1], func=AF.Sqrt, scale=1.0 / D, bias=RMS_EPS),
                 reads=[b_stA[i]], writes=[b_stA[i]])
            P.op("dve", lambda e: e.reciprocal(out=stA[0:nr, 34 + i:35 + i], in_=stA[0:nr, 17 + i:18 + i]),
                 reads=[b_stA[i]], writes=[b_stA[i]])
            P.op("dve", lambda e: e.scalar_tensor_tensor(out=xn[0:nr, :], in0=xin[0:nr, :], scalar=stA[0:nr, 34 + i:35 + i],
                                                         in1=GBC[0:nr, :], op0=ALU.mult, op1=ALU.mult),
                 reads=bx + [b_stA[i], b_gbc], writes=[bxn])

        pa_bank = {}

        def pa_stage2(i):
            nr, xin, bx, xn, bxn = pa_tile(i)
            pi = next_pc()
            pa_bank[i] = pi
            for k in range(8):
                P.op("pe", lambda e, k=k: e.transpose(out=PSb[pi][:, k * 128:k * 128 + nr], in_=xn[0:nr, k * 128:(k + 1) * 128],
                                                      identity=identb[0:nr, 0:nr]),
                     reads=[bxn, b_idb], writes=[b_ps[pi]])

        def pa_stage3(i):
            nr = 128 if i < 16 else 32
            r0 = 128 * i
            pi = pa_bank[i]
            src = PSb[pi][:, :].rearrange("p (k t) -> p k t", k=8)[:, :, 0:nr]
            P.op("dve", lambda e: e.tensor_copy(out=hT[:, :, r0:r0 + nr], in_=src),
                 reads=[b_ps[pi]], writes=[b_hT[min(i // 4, 4)]])

        for step in range(17 + 2):
            if step < 17:
                pa_stage1(step)
            if 1 <= step < 18:
                pa_stage2(step - 1)
            if 2 <= step < 19:
                pa_stage3(step - 2)
        P.op("dve", lambda e: e.memset(onesb[:], 1.0), writes=[b_idb])
        P.op("dve", lambda e: e.memset(SC1[:], 0.0), writes=b_acc)
        Prog.alias([b for row in b_v for b in row] + [b_mT], b_xa[1:] + [b_xa0] + b_xt[1:])

        def hT_bufs(lo, hi):
            return [b_hT[c] for c in range(lo // 512, min((hi - 1) // 512, 4) + 1)]

        def inproj_group(wv, bws, lo, n):
            pi = next_pp()
            for k in range(8):
                P.op("pe", lambda e, pi=pi, k=k, wv=wv, lo=lo, n=n: e.matmul(PS[pi][:, 0:n], lhsT=wv[:, k, :], rhs=hT[:, k, lo:lo + n],
                                                                            start=(k == 0), stop=(k == 7)),
                     reads=[bws] + hT_bufs(lo, lo + n), writes=[b_ps[pi]])
            return pi

        pending_stats = []

        def emit_stats(items):
            for (j, n, sqt, vbt, bsq, bvb) in items:
                P.op("pe", lambda e, vbt=vbt: e.matmul(PS[PS1][:, :], lhsT=onesb[:], rhs=vbt, start=True, stop=True),
                     reads=[bvb, b_idb], writes=[b_ps[PS1]])
                P.op("pe", lambda e, sqt=sqt: e.matmul(PS[PS2][:, :], lhsT=onesb[:], rhs=sqt, start=True, stop=True),
                     reads=[bsq, b_idb], writes=[b_ps[PS2]])
                P.op("dve", lambda e, n=n: e.tensor_tensor(out=acc1[:, n * 512:(n + 1) * 512], in0=PS[PS1][:, :], in1=acc1[:, n * 512:(n + 1) * 512], op=ALU.add),
                     reads=[b_ps[PS1], b_acc[n]], writes=[b_acc[n]])
                P.op("dve", lambda e, n=n: e.tensor_tensor(out=acc2[:, n * 512:(n + 1) * 512], in0=PS[PS2][:, :], in1=acc2[:, n * 512:(n + 1) * 512], op=ALU.add),
                     reads=[b_ps[PS2], b_acc[n]], writes=[b_acc[n]])

        chunks = [(c * 512, 512) for c in range(4)] + [(2048, 32)]

        def pair(j, g_first, g_second, func, dst, bdst, qbase, after_chunk=None):
            s1, wv1 = wload(w_in_r[g_first * 8 + j])
            s2, wv2 = wload(w_in_r[g_second * 8 + j])
            for ci, (lo, n) in enumerate(chunks):
                if after_chunk is not None and ci > 0:
                    after_chunk(ci - 1)
                tq = qbase + ci % 2
                p1 = inproj_group(wv1, b_ws[s1], lo, n)
                P.op("act", lambda e, p1=p1, n=n, tq=tq: e.activation(out=qf(tq)[:, 0:n], in_=PS[p1][:, 0:n], func=func, bias=col(c_bin(g_first, j))),
                     reads=[b_ps[p1], b_const], writes=[b_q[tq]])
                p2 = inproj_group(wv2, b_ws[s2], lo, n)
                P.op("dve", lambda e, p2=p2, n=n, lo=lo, tq=tq: e.scalar_tensor_tensor(
                    out=dst[:, lo:lo + n], in0=PS[p2][:, 0:n], scalar=col(c_bin(g_second, j)), in1=qf(tq)[:, 0:n], op0=ALU.add, op1=ALU.mult),
                     reads=[b_ps[p2], b_q[tq], b_const], writes=[bdst])
            P.op("dve", lambda e: e.tensor_tensor(out=dst[:, 0:16], in0=dst[:, 0:16], in1=hmask[:, 0:16], op=ALU.mult),
                 reads=[bdst, b_const], writes=[bdst])
            P.op("dve", lambda e: e.tensor_tensor(out=dst[:, NE - 16:NE], in0=dst[:, NE - 16:NE], in1=hmask[:, 16:32], op=ALU.mult),
                 reads=[bdst, b_const], writes=[bdst])

        def build_diagB(j):
            jb = j % 2
            P.op("pool", lambda e: e.tensor_tensor(out=W4[jb], in0=e4[:, :].unsqueeze(1).to_broadcast([128, 32, 32]),
                                                   in1=colw4[:, j * 32:(j + 1) * 32].unsqueeze(2).to_broadcast([128, 32, 32]), op=ALU.mult),
                 reads=[b_const], writes=[b_dg[jb]])

        def u4_dma(j):
            ub_ = uBs[j % 2]
            for s_ in range(4):
                for g in range(4):
                    P.dma("sp", lambda e, g=g, s_=s_: e.dma_start(out=U4[32 * s_:32 * s_ + 32, g, 0:NE - 8 * s_], in_=ub_[32 * g:32 * g + 32, 8 * s_:NE]),
                          s_u4[g], reads=[b_uB[j % 2]], writes=[b_U4[g]])

        def build_diagA(j):
            jb = j % 2
            P.op("pool", lambda e: e.tensor_tensor(out=diagA[jb], in0=identb[:, :].unsqueeze(1).to_broadcast([128, 3, 128]),
                                                   in1=col(c_cwA(j), 3).unsqueeze(2).to_broadcast([128, 3, 128]), op=ALU.mult),
                 reads=[b_idb, b_const], writes=[b_dgA[jb]])

        def pairB(j, after_chunk=None):
            pair(j, 4, 3, AF.Sigmoid, uBs[j % 2], b_uB[j % 2], 2, after_chunk)

        def pairA(j, after_chunk=None):
            pair(j, 2, 1, AF.Identity, uAs[j % 2], b_uA[j % 2], 0, after_chunk)

        def conv31(j):
            jb = j % 2
            items = []
            for n in range(4):
                e0 = HALO + 512 * n
                pc = next_pc()
                for tap0 in range(8):
                    for g in range(4):
                        P.op("pe", lambda e, pc=pc, tap0=tap0, g=g, e0=e0: e.matmul(PS[pc][32 * g:32 * g + 32, :], lhsT=W4[jb][:, g * 8 + tap0, :],
                                                                                 rhs=U4[:, g, e0 + tap0 - 15:e0 + tap0 - 15 + 512],
                                                                                 start=(tap0 == 0), stop=(tap0 == 7), tile_position=(0, 32 * g)),
                             reads=[b_dg[jb], b_U4[g]], writes=[b_ps[pc]])
                vt = vv[:, j, n * 512:(n + 1) * 512]
                sqt = qb(6 + n // 2)[:, (n % 2) * 512:(n % 2) * 512 + 512]
                P.op("act", lambda e, pc=pc, vt=vt: e.activation(out=vt, in_=PS[pc][:, :], func=AF.Identity, bias=col(c_cbB(j))),
                     reads=[b_ps[pc], b_const], writes=[b_v[j][n]])
                P.op("act", lambda e, pc=pc, sqt=sqt: e.activation(out=sqt, in_=PS[pc][:, :], func=AF.Square, bias=col(c_cbB(j))),
                     reads=[b_ps[pc], b_const], writes=[b_sq[n]])
                items.append((j, n, sqt, vt, b_sq[n], b_v[j][n]))
            return items

        def conv3_bg(j, after_chunk=None):
            jb = j % 2
            uA = uAs[jb]
            s0, wv0 = wload(w_in_r[0 * 8 + j])
            for n in range(4):
                e0 = HALO + 512 * n
                pc = next_pc()
                for tap in range(3):
                    P.op("pe", lambda e, pc=pc, tap=tap, e0=e0: e.matmul(PS[pc][:, :], lhsT=diagA[jb][:, tap, :], rhs=uA[:, e0 + tap - 1:e0 + tap - 1 + 512],
                                                                      start=(tap == 0), stop=(tap == 2)),
                         reads=[b_dgA[jb], b_uA[jb]], writes=[b_ps[pc]])
                tq = 4 + n % 2
                P.op("act", lambda e, pc=pc, tq=tq: e.activation(out=qf(tq), in_=PS[pc][:, :], func=AF.Identity, bias=col(c_cbA(j))),
                     reads=[b_ps[pc], b_const], writes=[b_q[tq]])
                p0 = inproj_group(wv0, b_ws[s0], e0, 512)
                P.op("dve", lambda e, p0=p0, tq=tq, n=n: e.scalar_tensor_tensor(out=pA[:, j, n * 512:(n + 1) * 512], in0=PS[p0][:, :], scalar=col(c_bin(0, j)),
                                                                             in1=qf(tq), op0=ALU.add, op1=ALU.mult),
                     reads=[b_ps[p0], b_q[tq], b_const], writes=[b_pA[j]])
                if after_chunk is not None:
                    after_chunk()

        pairB(0)
        build_diagB(0)
        u4_dma(0)
        prev_items = None
        for j in range(8):
            if j + 1 < 8:
                pi_ = prev_items
                pairB(j + 1, after_chunk=(lambda ci, pi_=pi_: emit_stats([pi_[ci]])) if pi_ else None)
                build_diagB(j + 1)
            else:
                pi_ = prev_items
                pairA(0, after_chunk=lambda ci, pi_=pi_: emit_stats([pi_[ci]]))
                build_diagA(0)
            prev_items = conv31(j)
            if j + 1 < 8:
                u4_dma(j + 1)
        last_items = prev_items

        bg = []

        def fin_a(n, step):
            a1 = acc1[:, n * 512:(n + 1) * 512]
            a2 = acc2[:, n * 512:(n + 1) * 512]
            mq = msq_tmp[:, n * 512:(n + 1) * 512]
            if step == 0:
                P.op("dve", lambda e: e.tensor_scalar(out=a1, in0=a1, scalar1=1.0 / D, scalar2=None, op0=ALU.mult), reads=[b_acc[n]], writes=[b_acc[n]])
            elif step == 1:
                P.op("dve", lambda e: e.tensor_tensor(out=mq, in0=a1, in1=a1, op=ALU.mult), reads=[b_acc[n]], writes=[b_mT])
            elif step == 2:
                P.op("dve", lambda e: e.scalar_tensor_tensor(out=a2, in0=a2, scalar=1.0 / D, in1=mq, op0=ALU.mult, op1=ALU.subtract),
                     reads=[b_acc[n], b_mT], writes=[b_acc[n]])
            else:
                P.op("act", lambda e: e.activation(out=a2, in_=a2, func=AF.Sqrt, bias=LN_EPS), reads=[b_acc[n]], writes=[b_acc[n]])

        def fin_n(n):
            a1 = acc1[:, n * 512:(n + 1) * 512]
            a2 = acc2[:, n * 512:(n + 1) * 512]
            mb = mu_bf[:, n * 512:(n + 1) * 512]
            rb = SC1b[:, n * 1024:n * 1024 + 512]
            P.op("dve", lambda e: e.tensor_copy(out=mb, in_=a1), reads=[b_acc[n]], writes=[b_gbc])
            P.op("dve", lambda e: e.tensor_copy(out=rb, in_=a2), reads=[b_acc[n], b_gbc], writes=[b_acc[n]])

        def ln_unit(n, j, part):
            mb = mu_bf[:, n * 512:(n + 1) * 512]
            rb = SC1b[:, n * 1024:n * 1024 + 512]
            tq = j % 2
            tt_ = JUNK[:, tq * 512:(tq + 1) * 512]
            vt = vv[:, j, n * 512:(n + 1) * 512]
            if part == 0:
                P.op("dve", lambda e: e.tensor_tensor(out=tt_, in0=vt, in1=mb, op=ALU.subtract), reads=[b_v[j][n], b_gbc], writes=[b_jk[tq]])
                P.op("dve", lambda e: e.tensor_tensor(out=tt_, in0=tt_, in1=rb, op=ALU.mult), reads=[b_jk[tq], b_acc[n]], writes=[b_jk[tq]])
            else:
                P.op("act", lambda e: e.activation(out=vt, in_=tt_, func=AF.Silu, scale=col(c_lng(j)), bias=col(c_lnb(j))),
                     reads=[b_jk[tq], b_const], writes=[b_v[j][n]])

        def recip_half(n, hh):
            a2h = acc2[:, n * 512 + hh * 256:n * 512 + (hh + 1) * 256]
            P.op("dve", lambda e: e.reciprocal(out=a2h, in_=a2h), reads=[b_acc[n]], writes=[b_acc[n]])

        for n in range(4):
            for step in range(4):
                bg.append(lambda n=n, step=step: fin_a(n, step))
            bg.append(lambda n=n: recip_half(n, 0))
            bg.append(lambda n=n: recip_half(n, 1))
            bg.append(lambda n=n: fin_n(n))
            for j in range(9):
                if j < 8:
                    bg.append(lambda n=n, j=j: ln_unit(n, j, 0))
                if j >= 1:
                    bg.append(lambda n=n, j=j: ln_unit(n, j - 1, 1))

        def drain(k):
            for _ in range(k):
                if bg:
                    bg.pop(0)()

        for j in range(8):
            if j + 1 < 8:
                if j == 0:
                    pairA(1, after_chunk=lambda ci: emit_stats([last_items[ci]]))
                else:
                    pairA(j + 1)
                build_diagA(j + 1)
            if j == 0:
                drain(8)
            conv3_bg(j, after_chunk=lambda: drain(3))
        drain(len(bg))

        Prog.alias([b_q[6], b_q[7]], b_sq)
        allv = [b for row in b_v for b in row]
        wo_pending = []
        oldB = b_uB + b_dg + b_dgA + b_U4
        for i in range(8):
            if i == 1 and DEBUG:
                s_d2 = sem("s_dbg2")
                P.dma("sp", lambda e: e.dma_start(out=dbg2, in_=R3[:, :]), s_d2, reads=allv)
                P.dma("sp", lambda e: e.dma_start(out=dbg3, in_=SC1[:, :]), s_d2, reads=allv + b_acc)
                P.dma("sp", lambda e: e.dma_start(out=dbg4, in_=GBC[:, :]), s_d2, reads=allv + [b_gbc])
            if i == 2:
                Prog.alias([b_wo, b_gp], oldB)
                wo_pending.extend(range(8))
                P.dma("sp", lambda e: e.dma_start(out=gp1, in_=gbc_d[1]), s_g, writes=[b_gp])
            for (zg, wsrc, act_src, bsrc_of, first) in ((5, wa_r, pA, lambda n: b_pA, True), (6, wb_r, vB, lambda n: [b_v[jj][n] for jj in range(8)], False)):
                sz, wvz = wload(w_in_r[zg * 8 + i])
                sy, wvy = wload(wsrc[i])
                for _ in range(2):
                    if wo_pending:
                        k_ = wo_pending.pop(0)
                        P.dma("pool", lambda e, k=k_: e.dma_start(out=w_o[:, k, :], in_=w_o_d[k * 128:(k + 1) * 128, :]), s_wo, writes=[b_wo])
                for n in range(4):
                    e0 = HALO + 512 * n
                    sq_ = n % 2
                    pz = inproj_group(wvz, b_ws[sz], e0, 512)
                    P.op("act", lambda e, pz=pz, sq_=sq_, zg=zg, i=i: e.activation(out=qf(sq_), in_=PS[pz][:, :], func=AF.Sigmoid, bias=col(c_bin(zg, i))),
                         reads=[b_ps[pz], b_const], writes=[b_q[sq_]])
                    py = next_pp()
                    for k in range(8):
                        P.op("pe", lambda e, py=py, k=k, wvy=wvy, act_src=act_src, n=n: e.matmul(PS[py][:, :], lhsT=wvy[:, k, :], rhs=act_src[:, k, n * 512:(n + 1) * 512],
                                                                                           start=(k == 0), stop=(k == 7)),
                             reads=[b_ws[sy]] + bsrc_of(n), writes=[b_ps[py]])
                    if first:
                        P.op("dve", lambda e, py=py, sq_=sq_, n=n: e.tensor_tensor(out=qf(2 + n), in0=PS[py][:, :], in1=qf(sq_), op=ALU.mult),
                             reads=[b_ps[py], b_q[sq_]], writes=[b_q[2 + n]])
                    else:
                        P.op("dve", lambda e, py=py, sq_=sq_, n=n: e.tensor_tensor(out=qf(6 + sq_), in0=PS[py][:, :], in1=qf(sq_), op=ALU.mult),
                             reads=[b_ps[py], b_q[sq_]], writes=[b_q[6 + sq_]])
                        P.op("dve", lambda e, sq_=sq_, n=n, i=i: e.tensor_tensor(out=mT[:, i, n * 512:(n + 1) * 512], in0=qf(6 + sq_), in1=qf(2 + n), op=ALU.add),
                             reads=[b_q[6 + sq_], b_q[2 + n]], writes=[b_mT])

        Prog.alias(b_x1, b_hT + b_pA)
        Prog.alias([b_h2T], b_acc)

        def H2_elem(tt):
            hq = 6 + tt % 2
            x1t = x1[:, tt, :]
            P.op("act", lambda e, x1t=x1t, tt=tt: e.activation(out=JUNK[:, :], in_=x1t, func=AF.Square, accum_out=sth[:, tt:tt + 1]),
                 reads=[b_x1[tt]], writes=[b_sth[tt]])
            P.op("act", lambda e, tt=tt: e.activation(out=sth[:, 16 + tt:17 + tt], in_=sth[:, tt:tt + 1], func=AF.Sqrt, scale=1.0 / D, bias=RMS_EPS),
                 reads=[b_sth[tt]], writes=[b_sth[tt]])
            P.op("dve", lambda e, tt=tt: e.reciprocal(out=sth[:, 32 + tt:33 + tt], in_=sth[:, 16 + tt:17 + tt]), reads=[b_sth[tt]], writes=[b_sth[tt]])
            P.op("dve", lambda e, x1t=x1t, tt=tt, hq=hq: e.scalar_tensor_tensor(out=qb(hq), in0=x1t, scalar=sth[:, 32 + tt:33 + tt], in1=gp2, op0=ALU.mult, op1=ALU.mult),
                 reads=[b_x1[tt], b_sth[tt], b_gp2], writes=[b_q[hq]])

        def H2_tr(tt):
            ttl = tt % 8
            hq = 6 + tt % 2
            pi = next_pc()
            for k in range(8):
                P.op("pe", lambda e, pi=pi, k=k, hq=hq: e.transpose(out=PSb[pi][:, k * 128:(k + 1) * 128], in_=qb(hq)[:, k * 128:(k + 1) * 128], identity=identb[:]),
                     reads=[b_q[hq], b_idb], writes=[b_ps[pi]])
            P.op("act", lambda e, pi=pi, ttl=ttl: e.activation(out=h2T[:, :, ttl * 128:(ttl + 1) * 128], in_=PSb[pi][:, :].rearrange("p (k t) -> p k t", k=8), func=AF.Copy),
                 reads=[b_ps[pi]], writes=[b_h2T])

        def s3_step(tt):
            s = tt % 2
            xin = qf(2 * s, 2)
            bx = [b_q[2 * s], b_q[2 * s + 1]]
            P.dma("sp", lambda e, xin=xin, tt=tt: e.dma_start(out=xin, in_=x_ext[HALO + tt * 128:HALO + (tt + 1) * 128, :]), s_x[s], writes=bx)
            pis = []
            for nh in range(2):
                pi = next_pp()
                pis.append(pi)
                for k in range(8):
                    P.op("pe", lambda e, pi=pi, k=k, tt=tt, nh=nh: e.matmul(PS[pi][:, :], lhsT=mT[:, k, tt * 128:(tt + 1) * 128], rhs=w_o[:, k, nh * 512:(nh + 1) * 512],
                                                                         start=(k == 0), stop=(k == 7)),
                         reads=[b_mT, b_wo], writes=[b_ps[pi]])
                P.op("act", lambda e, pi=pi, tt=tt, nh=nh: e.activation(out=JUNK[:, 0:512], in_=PS[pi][:, :], func=AF.Square, accum_out=st3[:, 2 * tt + nh:2 * tt + nh + 1]),
                     reads=[b_ps[pi]], writes=[b_st3[tt]])
            P.op("dve", lambda e, tt=tt: e.tensor_tensor(out=st3[:, 32 + tt:33 + tt], in0=st3[:, 2 * tt:2 * tt + 1], in1=st3[:, 2 * tt + 1:2 * tt + 2], op=ALU.add),
                 reads=[b_st3[tt]], writes=[b_st3[tt]])
            P.op("act", lambda e, tt=tt: e.activation(out=st3[:, 48 + tt:49 + tt], in_=st3[:, 32 + tt:33 + tt], func=AF.Sqrt, scale=1.0 / D, bias=RMS_EPS),
                 reads=[b_st3[tt]], writes=[b_st3[tt]])
            P.op("dve", lambda e, tt=tt: e.reciprocal(out=st3[:, 64 + tt:65 + tt], in_=st3[:, 48 + tt:49 + tt]), reads=[b_st3[tt]], writes=[b_st3[tt]])
            for nh in range(2):
                P.op("dve", lambda e, pi=pis[nh], tt=tt, nh=nh: e.scalar_tensor_tensor(out=x1[:, tt, nh * 512:(nh + 1) * 512], in0=PS[pi][:, :], scalar=st3[:, 64 + tt:65 + tt],
                                                                                in1=gp1[:, nh * 512:(nh + 1) * 512], op0=ALU.mult, op1=ALU.mult),
                     reads=[b_ps[pis[nh]], b_st3[tt], b_gp], writes=[b_x1[tt]])
            P.op("dve", lambda e, tt=tt, xin=xin: e.tensor_tensor(out=x1[:, tt, :], in0=x1[:, tt, :], in1=xin, op=ALU.add),
                 reads=[b_x1[tt]] + bx, writes=[b_x1[tt]])
            if 1 <= tt <= 8:
                H2_elem(tt - 1)
            if 2 <= tt <= 9:
                H2_tr(tt - 2)


        for tt in range(10):
            s3_step(tt)

        if DEBUG:
            s_d = sem("s_dbg")
            for tt in range(16):
                P.dma("sp", lambda e, tt=tt: e.dma_start(out=dbg1[tt * 128:(tt + 1) * 128, :], in_=x1[:, tt, :]), s_d, reads=[b_x1[tt]])
        P.dma("sp", lambda e: e.dma_start(out=GBC[:], in_=gbc_d[3]), s_gb, writes=[b_gbc])

        def mlp_in_block(h, m):
            s, wv = wload(w1_r[m])
            for n in range(2):
                pi = next_pp()
                for k in range(8):
                    P.op("pe", lambda e, pi=pi, k=k, wv=wv, n=n: e.matmul(PS[pi][:, :], lhsT=wv[:, k, :], rhs=h2T[:, k, n * 512:(n + 1) * 512], start=(k == 0), stop=(k == 7)),
                         reads=[b_ws[s], b_h2T], writes=[b_ps[pi]])
                rq = 6 + n
                P.op("act", lambda e, pi=pi, rq=rq: e.activation(out=qf(rq), in_=PS[pi][:, :], func=AF.Relu), reads=[b_ps[pi]], writes=[b_q[rq]])
                P.op("dve", lambda e, rq=rq, m=m, n=n: e.tensor_tensor(out=fT[:, m, n * 512:(n + 1) * 512], in0=qf(rq), in1=qf(rq), op=ALU.mult),
                     reads=[b_q[rq]], writes=[b_fT_hi if m >= 16 else b_fT_lo])

        def mlp_out_mm(h, hook=None, early_tail=None):
            for i in range(8):
                pis = [PP[(i % 2) * 2 + n] for n in range(2)]
                if early_tail is not None and i == 7:
                    pieces = [wload(w2_r[i * 4 + kq]) for kq in range(4)]
                    for n in range(2):
                        for kq in range(4):
                            s, wv = pieces[kq]
                            for kk in range(8):
                                P.op("pe", lambda e, pi=pis[n], wv=wv, kk=kk, kq=kq, n=n: e.matmul(PS[pi][:, :], lhsT=wv[:, kk, :], rhs=fT[:, kq * 8 + kk, n * 512:(n + 1) * 512],
                                                                                               start=(kq == 0 and kk == 0), stop=(kq == 3 and kk == 7)),
                                     reads=[b_ws[s], b_fT_lo, b_fT_hi], writes=[b_ps[pis[n]]])
                            if n == 1:
                                early_tail(kq)
                        if n == 0:
                            P.op("act", lambda e, pi=pis[0], i=i: e.activation(out=o2T[:, i, 0:512], in_=PS[pi][:, :], func=AF.Copy),
                                 reads=[b_ps[pis[0]]], writes=[b_o2T])
                    P.op("act", lambda e, pi=pis[1], i=i: e.activation(out=o2T[:, i, 512:1024], in_=PS[pi][:, :], func=AF.Copy),
                         reads=[b_ps[pis[1]]], writes=[b_o2T])
                    continue
                for kq in range(4):
                    s, wv = wload(w2_r[i * 4 + kq])
                    for n in range(2):
                        for kk in range(8):
                            P.op("pe", lambda e, pi=pis[n], wv=wv, kk=kk, kq=kq, n=n: e.matmul(PS[pi][:, :], lhsT=wv[:, kk, :], rhs=fT[:, kq * 8 + kk, n * 512:(n + 1) * 512],
                                                                                           start=(kq == 0 and kk == 0), stop=(kq == 3 and kk == 7)),
                                 reads=[b_ws[s], b_fT_lo, b_fT_hi], writes=[b_ps[pis[n]]])
                for n in range(2):
                    P.op("act", lambda e, pi=pis[n], i=i, n=n: e.activation(out=o2T[:, i, n * 512:(n + 1) * 512], in_=PS[pi][:, :], func=AF.Copy),
                         reads=[b_ps[pis[n]]], writes=[b_o2T])
                if hook is not None:
                    hook(i)

        def mlp_out_tail(h, ttl):
            tt = h * 8 + ttl
            banks = [PC[0], PC[1]] if ttl % 2 == 0 else [PS1, PS2]
            for i in range(8):
                pi = banks[i // 4]
                P.op("pe", lambda e, pi=pi, i=i, ttl=ttl: e.transpose(out=PS[pi][:, (i % 4) * 128:(i % 4 + 1) * 128], in_=o2T[:, i, ttl * 128:(ttl + 1) * 128], identity=identf[:]),
                     reads=[b_o2T, b_ident], writes=[b_ps[pi]])
            for x in range(2):
                P.op("act", lambda e, pi=banks[x], tt=tt, x=x: e.activation(out=JUNK[:, 0:512], in_=PS[pi][:, :], func=AF.Square, accum_out=sto[:, 2 * tt + x:2 * tt + x + 1]),
                     reads=[b_ps[banks[x]]], writes=[b_sto[tt]])
            P.op("dve", lambda e, tt=tt: e.tensor_tensor(out=sto[:, 32 + tt:33 + tt], in0=sto[:, 2 * tt:2 * tt + 1], in1=sto[:, 2 * tt + 1:2 * tt + 2], op=ALU.add),
                 reads=[b_sto[tt]], writes=[b_sto[tt]])
            P.op("act", lambda e, tt=tt: e.activation(out=sto[:, 48 + tt:49 + tt], in_=sto[:, 32 + tt:33 + tt], func=AF.Sqrt, scale=1.0 / D, bias=RMS_EPS),
                 reads=[b_sto[tt]], writes=[b_sto[tt]])
            P.op("dve", lambda e, tt=tt: e.reciprocal(out=sto[:, 64 + tt:65 + tt], in_=sto[:, 48 + tt:49 + tt]), reads=[b_sto[tt]], writes=[b_sto[tt]])
            s = tt % 3
            yout = qf(2 * s, 2)
            by = [b_q[2 * s], b_q[2 * s + 1]]
            for x in range(2):
                P.op("dve", lambda e, pi=banks[x], tt=tt, x=x, yout=yout: e.scalar_tensor_tensor(out=yout[:, x * 512:(x + 1) * 512], in0=PS[pi][:, :], scalar=sto[:, 64 + tt:65 + tt],
                                                                                          in1=GBC[:, x * 512:(x + 1) * 512], op0=ALU.mult, op1=ALU.mult),
                     reads=[b_ps[banks[x]], b_sto[tt], b_gbc], writes=by)
            P.op("pool" if (h == 1 and not (ttl >= 4 and ttl % 2 == 1)) else "dve", lambda e, tt=tt, yout=yout: e.tensor_tensor(out=yout, in0=yout, in1=x1[:, tt, :], op=ALU.add),
                 reads=by + [b_x1[tt]], writes=by)
            P.dma("sp", lambda e, tt=tt, yout=yout: e.dma_start(out=y[tt * 128:(tt + 1) * 128, :], in_=yout), s_o[s], reads=by)

        Prog.alias([b_fT_hi], allv)
        hi_blocks = list(range(16, 32))
        for tt in range(10, 16):
            s3_step(tt)
            for _ in range(3):
                if hi_blocks:
                    mlp_in_block(0, hi_blocks.pop(0))
        Prog.alias([b_fT_lo], [b_mT])
        Prog.alias([b_o2T], [b_wo, b_gp])
        for m in hi_blocks + list(range(16)):
            mlp_in_block(0, m)
        def hook0(i):
            if 8 + i + 1 <= 15:
                H2_elem(8 + i + 1)
            H2_tr(8 + i)

        H2_elem(8)
        mlp_out_mm(0, hook=hook0)
        for m in range(32):
            mlp_in_block(1, m)
            if 4 <= m < 12:
                mlp_out_tail(0, m - 4)
        mlp_out_mm(1, early_tail=lambda kq: mlp_out_tail(1, kq))
        for ttl in range(4, 8):
            mlp_out_tail(1, ttl)

        for i in range(3):
            P.final_waits.append((s_o[i], P.dma_counts[id(s_o[i])]))
        if DEBUG:
            P.final_waits.append((s_d, P.dma_counts[id(s_d)]))
            P.final_waits.append((s_d2, P.dma_counts[id(s_d2)]))
        P.emit(block, esems)
    return nc


def _blocks(w, nblk):
    return np.ascontiguousarray(w.reshape(8, 128, nblk, 128).transpose(2, 1, 0, 3).reshape(nblk, 128, 1024))


def _colmajor(v):
    return v.reshape(-1, 128).T


_NC_CACHE = {}


def kernel(x, norm1_pre_g, w_in, b_in, conv_a_w, conv_a_b, w_a_out, conv_b_w, conv_b_b, ln_b_g, ln_b_b,
           w_b_out, w_o, norm1_post_g, norm2_pre_g, w_mlp_in, w_mlp_out, norm2_post_g):
    f = lambda a: np.asarray(a, dtype=np.float32)
    x = f(x)
    w_in = f(w_in)
    B, S, _ = x.shape
    w_in_r = np.ascontiguousarray(w_in.reshape(8, 128, 7, 8, 128).transpose(2, 3, 1, 0, 4).reshape(56, 128, 1024))
    wa_r = _blocks(f(w_a_out), 8)
    wb_r = _blocks(f(w_b_out), 8)
    w1_r = _blocks(f(w_mlp_in), 32)
    w2_r = np.ascontiguousarray(f(w_mlp_out).reshape(4, 8, 128, 8, 128).transpose(3, 0, 2, 1, 4).reshape(32, 128, 1024))
    colv = np.zeros((128, NCOL), np.float32)
    colv[:, 0:56] = _colmajor(f(b_in))
    colv[:, 56:64] = _colmajor(f(conv_a_b))
    colv[:, 64:72] = _colmajor(f(conv_b_b))
    colv[:, 72:80] = _colmajor(f(ln_b_g))
    colv[:, 80:88] = _colmajor(f(ln_b_b))
    cwa = f(conv_a_w)
    cwb = f(conv_b_w)
    colv[:, 88:112] = cwa.reshape(3, 8, 128).transpose(2, 1, 0).reshape(128, 24)
    cwb_pad = np.concatenate([cwb, np.zeros((1, D), np.float32)], axis=0)
    colw4 = np.ascontiguousarray(cwb_pad.reshape(4, 8, 8, 4, 32).transpose(0, 4, 2, 3, 1).reshape(128, 256))
    e4 = np.ascontiguousarray(np.tile(np.eye(32, dtype=np.float32), (4, 1)))
    gbc = np.ascontiguousarray(np.broadcast_to(
        np.stack([f(norm1_pre_g), f(norm1_post_g), f(norm2_pre_g), f(norm2_post_g)])[:, None, :], (4, 128, D)))
    ident = np.eye(128, dtype=np.float32)
    w_o = np.ascontiguousarray(f(w_o))

    in_maps = []
    nq = S // NT
    for c in range(8):
        b, q = c // nq, c % nq
        t0 = q * NT
        xe = np.zeros((NE, D), np.float32)
        lo, hi = max(t0 - HALO, 0), min(t0 + NT + HALO, S)
        xe[lo - (t0 - HALO):hi - (t0 - HALO)] = x[b, lo:hi]
        hm = np.zeros((128, 32), np.float32)
        if q > 0:
            hm[:, 0:16] = 1.0
        if q < nq - 1:
            hm[:, 16:32] = 1.0
        in_maps.append({"x_ext": xe, "hmask": hm, "w_in_r": w_in_r, "wa_r": wa_r, "wb_r": wb_r, "w_o": w_o,
                        "w1_r": w1_r, "w2_r": w2_r, "colv": colv, "gbc": gbc, "ident": ident, "colw4": colw4, "e4": e4})
    if "nc" not in _NC_CACHE:
        _NC_CACHE["nc"] = build_nc()
    nc = _NC_CACHE["nc"]
    res = run_bass_kernel_spmd(nc, in_maps, core_ids=list(range(8)))
    if DEBUG:
        _LAST["res"] = res
    out = np.empty((B, S, D), np.float32)
    for c in range(8):
        b, q = c // nq, c % nq
        out[b, q * NT:(q + 1) * NT] = res.results[c]["y"]
    return out
```
